# Optimizing a Trainium2 kernel written in Bass

```python
import math
import jax, jax.numpy as jnp
from jax import lax
import numpy as np

D_MODEL = 1024
BATCH = 8
SEQ = 2048
DEPTH = 1
DEC_BATCH = 128
DEC_SEQ = 8
PAST_LEN = 16384
PAGE_SIZE = 128

MIX_WIDTH = D_MODEL
A_GROUPS = 4
A_WIDTH = MIX_WIDTH // 2
A_CH = A_WIDTH // A_GROUPS
CHUNK = 128
R_HEADS = 4
R_WIDTH = MIX_WIDTH - A_WIDTH
R_DK = R_WIDTH // R_HEADS
R_DV = R_WIDTH // R_HEADS
RET_CHUNK = 128
ROPE_BASE = 10000.0
IN_WIDTH = 2 * A_WIDTH + 4 * R_WIDTH
N_MEM = 256
CA_HEADS = 4
CA_DH = D_MODEL // CA_HEADS
PEER_HEADS = 8
PEER_KEYS = 128
PEER_EXPERTS = PEER_KEYS * PEER_KEYS
PEER_TOPK = 16
PEER_DKEY = 256
PEER_DHALF = PEER_DKEY // 2
PEER_BLOCK = 128
ALPHA = (2 * DEPTH) ** 0.25
BETA = (8 * DEPTH) ** -0.25
LN_EPS = 1e-5

kernel_name = 'hybrid_gmlp_retention_peer_step'


def layer_norm(x, g, b):
    xf = x.astype(jnp.float32)
    mu = xf.mean(-1, keepdims=True)
    var = jnp.square(xf - mu).mean(-1, keepdims=True)
    return ((xf - mu) * lax.rsqrt(var + LN_EPS)).astype(x.dtype) * g + b


def rotary(x, pos):
    half = x.shape[-1] // 2
    inv = 1.0 / (ROPE_BASE ** (jnp.arange(half, dtype=jnp.float32) / half))
    ang = pos.astype(jnp.float32)[:, None] * inv[None, :]
    cos = jnp.cos(ang)[None, :, None, :]
    sin = jnp.sin(ang)[None, :, None, :]
    x1, x2 = x[..., :half], x[..., half:]
    return jnp.concatenate([x1 * cos - x2 * sin, x1 * sin + x2 * cos], axis=-1)


def spatial_gate(u, v, w_s, b_s, g, b):
    B, L = v.shape[0], v.shape[1]
    v = layer_norm(v, g, b)
    Lp = -(-L // CHUNK) * CHUNK
    vp = jnp.pad(v, ((0, 0), (0, Lp - L), (0, 0), (0, 0))).reshape(B, Lp // CHUNK, CHUNK, A_GROUPS, A_CH)
    ws = w_s * jnp.tril(jnp.ones((CHUNK, CHUNK), w_s.dtype))
    mixed = jnp.einsum('gts,bnsgc->bntgc', ws, vp) + b_s.T[None, None, :, :, None]
    mixed = mixed.reshape(B, Lp, A_GROUPS, A_CH)[:, :L]
    return u * mixed, v


def retention(q, k, v, s0):
    B, L, H = q.shape[0], q.shape[1], q.shape[2]
    C = RET_CHUNK if L % RET_CHUNK == 0 else L
    n = L // C
    lg = jnp.log(1.0 - 2.0 ** (-5.0 - jnp.arange(H, dtype=jnp.float32)))
    i = jnp.arange(C, dtype=jnp.float32)
    diff = i[:, None] - i[None, :]
    dmask = jnp.where(diff[None] >= 0, jnp.exp(jnp.maximum(diff, 0.0)[None] * lg[:, None, None]), 0.0)
    q_dec = jnp.exp((i + 1.0)[:, None] * lg[None, :])
    k_dec = jnp.exp((C - 1.0 - i)[:, None] * lg[None, :])
    c_dec = jnp.exp(C * lg)

    def step(S, blk):
        qc, kc, vc = blk
        inner = jnp.einsum('bihd,bjhd->bhij', qc, kc) * dmask
        o = (jnp.einsum('bhij,bjhe->bihe', inner, vc)
             + jnp.einsum('bihd,bhde->bihe', qc, S) * q_dec[None, :, :, None])
        S = S * c_dec[None, :, None, None] + jnp.einsum('bjhd,bjhe->bhde', kc * k_dec[None, :, :, None], vc)
        return S, o

    def split(t):
        return t.reshape(B, n, C, H, t.shape[-1]).transpose(1, 0, 2, 3, 4)

    S, o = lax.scan(step, s0, (split(q), split(k), split(v)))
    o = o.transpose(1, 0, 2, 3, 4).reshape(B, L, H, v.shape[-1])
    return o, S


def peer(x, wq, subkeys, u_tab, v_tab):
    T, D = x.shape
    pad = (-T) % PEER_BLOCK
    xb = jnp.pad(x, ((0, pad), (0, 0))).reshape(-1, PEER_BLOCK, D)
    ncand = PEER_TOPK * PEER_TOPK

    def block(xt):
        q = (xt @ wq).reshape(PEER_BLOCK, PEER_HEADS, 2, PEER_DHALF)
        s = jnp.einsum('thcd,hckd->thck', q, subkeys).astype(jnp.float32)
        sv, si = lax.top_k(s, PEER_TOPK)
        cand = (sv[:, :, 0, :, None] + sv[:, :, 1, None, :]).reshape(PEER_BLOCK, PEER_HEADS, ncand)
        cidx = (si[:, :, 0, :, None] * PEER_KEYS + si[:, :, 1, None, :]).reshape(PEER_BLOCK, PEER_HEADS, ncand)
        fv, fi = lax.top_k(cand, PEER_TOPK)
        eidx = jnp.take_along_axis(cidx, fi, axis=-1)
        g = jax.nn.softmax(fv, axis=-1).astype(xt.dtype)
        act = jax.nn.gelu(jnp.einsum('thed,td->the', u_tab[eidx], xt))
        return jnp.einsum('the,thed->td', g * act, v_tab[eidx])

    return lax.map(block, xb).reshape(-1, D)[:T]


def layer(x, pos0, mem_k, mem_v, ret_s0, w_in, w_s, b_s, gate_ln_g, gate_ln_b, ret_gn_g, ret_gn_b,
          w_o, ln1_g, ln1_b, ca_wq, ca_wo, ln2_g, ln2_b, peer_wq, peer_subkeys, peer_u, peer_v,
          ln3_g, ln3_b):
    B, L, D = x.shape
    f32 = jnp.float32
    h = x @ w_in
    cuts = [A_WIDTH, 2 * A_WIDTH, 2 * A_WIDTH + R_WIDTH, 2 * A_WIDTH + 2 * R_WIDTH, 2 * A_WIDTH + 3 * R_WIDTH]
    ua, va, qr, kr, vr, gr = jnp.split(h, cuts, axis=-1)
    ua = jax.nn.gelu(ua).reshape(B, L, A_GROUPS, A_CH)
    va = jax.nn.gelu(va).reshape(B, L, A_GROUPS, A_CH)
    a_out, va_n = spatial_gate(ua, va, w_s, b_s, gate_ln_g, gate_ln_b)
    pos = pos0 + jnp.arange(L, dtype=jnp.int32)
    q = rotary(qr.reshape(B, L, R_HEADS, R_DK).astype(f32), pos)
    k = rotary(kr.reshape(B, L, R_HEADS, R_DK).astype(f32), pos) * (R_DK ** -0.5)
    v = vr.reshape(B, L, R_HEADS, R_DV).astype(f32)
    o, s_new = retention(q, k, v, ret_s0.astype(f32))
    o = layer_norm(o, ret_gn_g.astype(f32), ret_gn_b.astype(f32)).astype(x.dtype)
    r_out = jax.nn.silu(gr) * o.reshape(B, L, R_WIDTH)
    mix = jnp.concatenate([a_out.reshape(B, L, A_WIDTH), r_out], axis=-1) @ w_o
    x = layer_norm(ALPHA * x + mix, ln1_g, ln1_b)
    qc = (x @ ca_wq).reshape(B, L, CA_HEADS, CA_DH)
    sc = jnp.einsum('blhd,bmhd->bhlm', qc, mem_k).astype(f32) * (CA_DH ** -0.5)
    p = jax.nn.softmax(sc, axis=-1).astype(x.dtype)
    ca = jnp.einsum('bhlm,bmhd->blhd', p, mem_v).reshape(B, L, D) @ ca_wo
    x = layer_norm(ALPHA * x + ca, ln2_g, ln2_b)
    f = peer(x.reshape(B * L, D), peer_wq, peer_subkeys, peer_u, peer_v).reshape(B, L, D)
    x = layer_norm(ALPHA * x + f, ln3_g, ln3_b)
    return x, s_new.astype(ret_s0.dtype), va_n


def setup_inputs(seed: int = 0) -> dict:
    key = jax.random.key(seed)
    ks = jax.random.split(key, 32)
    f32 = jnp.float32

    def nrm(i, shape, scale):
        return jax.random.normal(ks[i], shape, f32) * scale

    def gain(i, shape):
        return 1.0 + nrm(i, shape, 0.02)

    D = D_MODEL
    return {
        'x_prompt': nrm(0, (BATCH, SEQ, D), 1.0),
        'x_sample': nrm(1, (DEC_BATCH, DEC_SEQ, D), 1.0),
        'mem_prompt': nrm(2, (BATCH, N_MEM, D), 1.0),
        'cache_mem_k': nrm(3, (DEPTH, DEC_BATCH, N_MEM, CA_HEADS, CA_DH), 1.0),
        'cache_mem_v': nrm(4, (DEPTH, DEC_BATCH, N_MEM, CA_HEADS, CA_DH), 1.0),
        'state_ret': nrm(5, (DEPTH, DEC_BATCH, R_HEADS, R_DK, R_DV), 0.1),
        'w_in': nrm(6, (DEPTH, D, IN_WIDTH), D ** -0.5),
        'w_s': nrm(7, (DEPTH, A_GROUPS, CHUNK, CHUNK), CHUNK ** -0.5),
        'b_s': gain(8, (DEPTH, A_GROUPS, CHUNK)),
        'gate_ln_g': gain(9, (DEPTH, A_GROUPS, A_CH)),
        'gate_ln_b': nrm(10, (DEPTH, A_GROUPS, A_CH), 0.02),
        'ret_gn_g': gain(11, (DEPTH, R_HEADS, R_DV)),
        'ret_gn_b': nrm(12, (DEPTH, R_HEADS, R_DV), 0.02),
        'w_o': nrm(13, (DEPTH, MIX_WIDTH, D), BETA * MIX_WIDTH ** -0.5),
        'ln1_g': gain(14, (DEPTH, D)),
        'ln1_b': nrm(15, (DEPTH, D), 0.02),
        'ca_wq': nrm(16, (DEPTH, D, D), D ** -0.5),
        'ca_wk': nrm(17, (DEPTH, D, D), D ** -0.5),
        'ca_wv': nrm(18, (DEPTH, D, D), D ** -0.5),
        'ca_wo': nrm(19, (DEPTH, D, D), BETA * D ** -0.5),
        'ln2_g': gain(20, (DEPTH, D)),
        'ln2_b': nrm(21, (DEPTH, D), 0.02),
        'peer_wq': nrm(22, (DEPTH, D, PEER_HEADS * PEER_DKEY), D ** -0.5),
        'peer_subkeys': nrm(23, (DEPTH, PEER_HEADS, 2, PEER_KEYS, PEER_DHALF), PEER_DHALF ** -0.5),
        'peer_u': nrm(24, (DEPTH, PEER_EXPERTS, D), D ** -0.5),
        'peer_v': nrm(25, (DEPTH, PEER_EXPERTS, D), BETA * PEER_HEADS ** -0.5),
        'ln3_g': gain(26, (DEPTH, D)),
        'ln3_b': nrm(27, (DEPTH, D), 0.02),
    }


def reference(x_prompt, x_sample, mem_prompt, cache_mem_k, cache_mem_v, state_ret, w_in, w_s, b_s,
              gate_ln_g, gate_ln_b, ret_gn_g, ret_gn_b, w_o, ln1_g, ln1_b, ca_wq, ca_wk, ca_wv, ca_wo,
              ln2_g, ln2_b, peer_wq, peer_subkeys, peer_u, peer_v, ln3_g, ln3_b):
    yp, ys = x_prompt, x_sample
    bp = x_prompt.shape[0]
    mk_l, mv_l, sp_l, ss_l, gv_l = [], [], [], [], []
    for l in range(DEPTH):
        lw = (w_in[l], w_s[l], b_s[l], gate_ln_g[l], gate_ln_b[l], ret_gn_g[l], ret_gn_b[l], w_o[l],
              ln1_g[l], ln1_b[l], ca_wq[l], ca_wo[l], ln2_g[l], ln2_b[l], peer_wq[l], peer_subkeys[l],
              peer_u[l], peer_v[l], ln3_g[l], ln3_b[l])
        mem_k = (mem_prompt @ ca_wk[l]).reshape(bp, N_MEM, CA_HEADS, CA_DH)
        mem_v = (mem_prompt @ ca_wv[l]).reshape(bp, N_MEM, CA_HEADS, CA_DH)
        s0 = jnp.zeros((bp, R_HEADS, R_DK, R_DV), x_prompt.dtype)
        yp, sp, _ = layer(yp, 0, mem_k, mem_v, s0, *lw)
        ys, ss, gv = layer(ys, PAST_LEN, cache_mem_k[l], cache_mem_v[l], state_ret[l], *lw)
        mk_l.append(mem_k)
        mv_l.append(mem_v)
        sp_l.append(sp)
        ss_l.append(ss)
        gv_l.append(gv)
    new_mem_k = jnp.stack(mk_l)
    new_mem_v = jnp.stack(mv_l)
    new_ret_prompt = jnp.stack(sp_l)
    new_ret_sample = jnp.stack(ss_l)
    new_gate_v_sample = jnp.stack(gv_l)
    return (yp, ys, new_mem_k, new_mem_v, new_ret_prompt, new_ret_sample, new_gate_v_sample)
```

```python
import os
import math
from contextlib import ExitStack
import numpy as np
import ml_dtypes
import concourse.bass as bass
import concourse.mybir as mybir
from concourse.bass_utils import run_bass_kernel_spmd

F32 = mybir.dt.float32
BF16 = mybir.dt.bfloat16
AF = mybir.ActivationFunctionType
ALU = mybir.AluOpType
AX = mybir.AxisListType

NT = 17
ALPHA = 2.0 ** 0.25
EPS = 1e-5
STAGE = int(os.environ.get("MK_STAGE", "9"))


class View:
    __slots__ = ("buf", "ap")

    def __init__(self, buf, ap):
        self.buf = buf
        self.ap = ap

    def __getitem__(self, idx):
        return View(self.buf, self.ap[idx])

    def bc(self, shape):
        return View(self.buf, self.ap.broadcast_to(list(shape)))

    def uq(self, axis):
        return View(self.buf, self.ap.unsqueeze(axis))

    def re(self, pat, **kw):
        return View(self.buf, self.ap.rearrange(pat, **kw))


class Buf:
    __slots__ = ("t", "name", "lw", "rd", "dsem", "dcnt", "psum", "is_dram", "wtoks")

    def __init__(self, t, name, psum=False, is_dram=False):
        self.t = t
        self.name = name
        self.psum = psum
        self.is_dram = is_dram
        self.wtoks = []
        self.lw = None
        self.rd = []
        self.dsem = None
        self.dcnt = 0

    def __getitem__(self, idx):
        return View(self, self.t[idx])


class K:
    SEM_LIMIT = 30000

    def __init__(self, nc):
        self.nc = nc
        self.eng = {"pe": nc.tensor, "act": nc.scalar, "dve": nc.vector, "pool": nc.gpsimd, "sp": nc.sync}
        self.esem, self.ecnt = {}, {}
        self.known = {e: {} for e in self.eng}
        self.nsem = 0
        self.allsems = []
        for e in self.eng:
            self.esem[e] = self._newsem("e_" + e)
            self.ecnt[e] = 0
        self.dbufs = []
        self.stacks = []
        self.nalloc = 0
        self.scope_bufs = []
        self.sem_by_name = {}

    def _newsem(self, name):
        self.nsem += 1
        s = self.nc.alloc_semaphore(name=f"{name}_{self.nsem}")
        self.allsems.append(s)
        return s

    def push(self):
        self.stacks.append(ExitStack())
        self.scope_bufs.append([])

    def pop(self):
        self.barrier()
        self.stacks.pop().close()
        for b in self.scope_bufs.pop():
            if b.dsem is not None:
                self.sem_by_name[b.name.rsplit("_s", 1)[0]] = (b.dsem, b.dcnt)
                self.dbufs.remove(b)
                b.dsem = None

    def sb(self, name, shape, dt=F32):
        self.nalloc += 1
        name = f"{name}_s{self.nalloc}"
        t = self.stacks[-1].enter_context(self.nc.sbuf_tensor(name, list(shape), dt))
        b = Buf(t, name)
        self.scope_bufs[-1].append(b)
        return b

    def ps(self, name, shape, dt=F32):
        self.nalloc += 1
        name = f"{name}_p{self.nalloc}"
        t = self.stacks[-1].enter_context(self.nc.psum_tensor(name, list(shape), dt))
        return Buf(t, name, psum=True)

    def dram(self, name, shape, dt=F32, kind="Internal"):
        return Buf(self.nc.dram_tensor(name, list(shape), dt, kind=kind).ap(), name, is_dram=True)

    def _wait(self, e, deps):
        kn = self.known[e]
        best = {}
        for (sem, val, _) in deps:
            key = id(sem)
            if kn.get(key, 0) >= val:
                continue
            if key not in best or best[key][1] < val:
                best[key] = (sem, val)
        for key, (sem, val) in best.items():
            self.eng[e].wait_ge(sem, val)
            kn[key] = val

    def op(self, e, fn, reads=(), writes=(), pe_accum=False):
        deps = []
        for b in reads:
            if b.lw is not None:
                deps.append(b.lw)
            if b.psum:
                deps.extend(t_ for t_ in b.rd if t_[2] != e)
        for b in writes:
            if b.lw is not None and not (pe_accum and b.lw[2] == "pe"):
                deps.append(b.lw)
            deps.extend(b.rd)
        self._wait(e, deps)
        if self.ecnt[e] >= self.SEM_LIMIT:
            self.esem[e] = self._newsem("e_" + e)
            self.ecnt[e] = 0
        ins = fn()
        self.ecnt[e] += 1
        ins.then_inc(self.esem[e], 1)
        tok = (self.esem[e], self.ecnt[e], e)
        for b in reads:
            if len(b.rd) > 64:
                b.rd = b.rd[-48:]
            b.rd.append(tok)
        for b in writes:
            b.lw = tok
            b.rd = []
        return ins

    def dma(self, q, out, in_, **kw):
        src, dst = in_.buf, out.buf
        deps = []
        if src.lw is not None:
            deps.append(src.lw)
        deps.extend(src.wtoks)
        if dst.lw is not None and dst.lw[2] != "dma":
            deps.append(dst.lw)
        deps.extend(dst.rd)
        self._wait(q, deps)
        own = src if (dst.is_dram and not src.is_dram) else dst
        if own.dsem is None:
            base = own.name.rsplit("_s", 1)[0]
            if base in self.sem_by_name:
                own.dsem, own.dcnt = self.sem_by_name.pop(base)
            else:
                own.dsem = self._newsem("d_" + own.name)
            self.dbufs.append(own)
        ins = self.eng[q].dma_start(out=out.ap, in_=in_.ap, **kw)
        own.dcnt += 16
        ins.then_inc(own.dsem, 16)
        tok = (own.dsem, own.dcnt, "dma")
        src.rd.append(tok)
        if own is src:
            dst.wtoks.append(tok)
            if len(dst.wtoks) > 40:
                dst.wtoks = dst.wtoks[-40:]
        else:
            dst.lw = tok
            dst.rd = []
        return ins

    def barrier(self, engines=None):
        deps = [(self.esem[e], self.ecnt[e], e) for e in self.eng if self.ecnt[e] > 0]
        deps += [(b.dsem, b.dcnt, "dma") for b in self.dbufs if b.dcnt > 0]
        for e in (engines or self.eng):
            self._wait(e, deps)

    @staticmethod
    def _bufs(*xs):
        return [x.buf for x in xs if isinstance(x, View)]

    @staticmethod
    def _a(x):
        return x.ap if isinstance(x, View) else x

    def tt(self, e, out, a, b, op):
        E = self.eng[e]
        return self.op(e, lambda: E.tensor_tensor(out=out.ap, in0=a.ap, in1=b.ap, op=op), self._bufs(a, b), [out.buf])

    def ts(self, e, out, a, s1, op0, s2=None, op1=None):
        E = self.eng[e]
        kw = {} if op1 is None else {"op1": op1}
        return self.op(e, lambda: E.tensor_scalar(out=out.ap, in0=a.ap, scalar1=self._a(s1), scalar2=self._a(s2), op0=op0, **kw),
                       self._bufs(a, s1, s2), [out.buf])

    def stt(self, e, out, a, s, b, op0, op1):
        E = self.eng[e]
        return self.op(e, lambda: E.scalar_tensor_tensor(out=out.ap, in0=a.ap, scalar=self._a(s), in1=b.ap, op0=op0, op1=op1),
                       self._bufs(a, s, b), [out.buf])

    def cp(self, e, out, a):
        E = self.eng[e]
        if e == "act":
            return self.op(e, lambda: E.copy(out=out.ap, in_=a.ap), [a.buf], [out.buf])
        return self.op(e, lambda: E.tensor_copy(out=out.ap, in_=a.ap), [a.buf], [out.buf])

    def act(self, out, a, func, bias=None, scale=None, accum=None, alpha=None):
        kw = {}
        if alpha is not None:
            kw["alpha"] = alpha
        if bias is not None:
            kw["bias"] = self._a(bias)
        if scale is not None:
            kw["scale"] = self._a(scale)
        w = [out.buf]
        if accum is not None:
            kw["accum_out"] = accum.ap
            w.append(accum.buf)
        return self.op("act", lambda: self.nc.scalar.activation(out=out.ap, in_=a.ap, func=func, **kw),
                       self._bufs(a, bias, scale), w)

    def red(self, out, a, op):
        return self.op("dve", lambda: self.nc.vector.tensor_reduce(out=out.ap, in_=a.ap, axis=AX.X, op=op), [a.buf], [out.buf])

    def recip(self, out, a):
        return self.op("dve", lambda: self.nc.vector.reciprocal(out=out.ap, in_=a.ap), [a.buf], [out.buf])

    def memset(self, e, out, val):
        E = self.eng[e]
        return self.op(e, lambda: E.memset(out.ap, val), [], [out.buf])

    def mm(self, out, lhsT, rhs, start, stop, skip=False):
        return self.op("pe", lambda: self.nc.tensor.matmul(out.ap, lhsT=lhsT.ap, rhs=rhs.ap, start=start, stop=stop, skip_group_check=skip),
                       [lhsT.buf, rhs.buf], [out.buf], pe_accum=True)

    def tr(self, out, a, ident):
        return self.op("pe", lambda: self.nc.tensor.transpose(out=out.ap, in_=a.ap, identity=ident.ap),
                       [a.buf, ident.buf], [out.buf], pe_accum=True)

    def max8(self, out, a):
        return self.op("dve", lambda: self.nc.vector.max(out=out.ap, in_=a.ap), [a.buf], [out.buf])

    def mrep(self, out, rep, a, imm):
        return self.op("dve", lambda: self.nc.vector.match_replace(out=out.ap, in_to_replace=rep.ap, in_values=a.ap, imm_value=imm),
                       [rep.buf, a.buf], [out.buf])


def group_ln(k, src, G, C, gt, bt, out, tmp, st):
    s1, nm, s2, rs = st[:, 0, :], st[:, 1, :], st[:, 2, :], st[:, 3, :]
    k.red(s1, src, ALU.add)
    k.ts("dve", nm, s1, -1.0 / C, ALU.mult)
    k.tt("dve", tmp, src, nm.uq(2).bc([128, G, C]), ALU.add)
    k.tt("pool", out, tmp, tmp, ALU.mult)
    k.red(s2, out, ALU.add)
    k.act(rs, s2, AF.Sqrt, bias=EPS, scale=1.0 / C)
    k.recip(rs, rs)
    k.tt("dve", tmp, tmp, rs.uq(2).bc([128, G, C]), ALU.mult)
    k.tt("pool", tmp, tmp, gt, ALU.mult)
    k.tt("dve", out, tmp, bt, ALU.add)


def build():
    nc = bass.Bass("TRN2", target_bir_lowering=False)
    k = K(nc)
    D = k.dram
    EI, EO = "ExternalInput", "ExternalOutput"
    xin = D("xin", [NT, 128, 1024], F32, EI)
    xinT = D("xinT", [NT, 128, 8, 128], F32, EI)
    mTl = D("mTl", [128, 8, 256], F32, EI)
    ckT = D("ckT", [16, 128, 8, 256], F32, EI)
    cvl = D("cvl", [16, 128, 2, 1024], F32, EI)
    sret = D("sret", [128, 16, 4, 128], F32, EI)
    w_in = D("w_in", [128, 8, 3072], F32, EI)
    w_o = D("w_o", [128, 8, 1024], F32, EI)
    ca_wq = D("ca_wq", [128, 8, 1024], F32, EI)
    ca_wk = D("ca_wk", [128, 8, 1024], F32, EI)
    ca_wv = D("ca_wv", [128, 8, 1024], F32, EI)
    ca_wo = D("ca_wo", [128, 8, 1024], F32, EI)
    peer_wq = D("peer_wq", [128, 8, 2048], F32, EI)
    skTl = D("skTl", [128, 16, 128], F32, EI)
    NCH = 128 if STAGE >= 4 else 1
    UTl = D("UTl", [NCH, 128, 8, 128], F32, EI)
    peer_v = D("peer_v", [NCH, 128, 1024], F32, EI)
    wsTl = D("wsTl", [2, 128, 4, 128], F32, EI)
    wsmask = D("wsmask", [2, 128, 4, 128], F32, EI)
    bsl = D("bsl", [2, 128, 4], F32, EI)
    vecs = D("vecs", [10, 1024], F32, EI)
    rope = D("rope", [NT, 128, 4, 256], F32, EI)
    dmaskT = D("dmaskT", [2, 128, 4, 128], F32, EI)
    kdec = D("kdec", [2, 128, 4], F32, EI)
    ident_d = D("ident", [128, 128], BF16, EI)
    y = D("y", [NT, 128, 1024], F32, EO)
    omk = D("omk", [256, 1024], F32, EO)
    omv = D("omv", [256, 1024], F32, EO)
    orp = D("orp", [4, 128, 128], F32, EO)
    ors = D("ors", [16, 4, 128, 128], F32, EO)
    ogv = D("ogv", [128, 512], F32, EO)
    DBG = EO if STAGE < 9 else "Internal"
    x1d = D("x1d", [NT, 128, 1024], F32, DBG)
    x2d = D("x2d", [NT, 128, 1024], F32, DBG)
    sd = D("sd", [NT, 128, 2048], F32)
    outs = [y, omk, omv, orp, ors, ogv]

    gam = [1.0 - 2.0 ** (-5.0 - h) for h in range(4)]

    k.push()
    ident = k.sb("ident", [128, 128], BF16)
    k.dma("sp", ident[:], ident_d[:])
    zeros = k.sb("zeros", [128, 512], BF16)
    k.memset("pool", zeros[:], 0.0)

    bnst = k.sb("bnst", [128, 2, 6], F32)
    bnag = k.sb("bnag", [128, 4], F32)

    def load_vec(dst, i, n):
        k.dma("sp", dst, View(vecs, vecs.t[i, 0:n].partition_broadcast(128)))

    def load_w(dst, src, ncols):
        for kc in range(8):
            for n0 in range(0, ncols, 1024):
                k.dma("pool", dst[:, kc, n0:n0 + 1024], src[:, kc, n0:n0 + 1024])

    def final_ln(ti, xres, pmix, lnv, out_dram, pre, tmp, st):
        for n_ in range(2):
            k.stt("dve", pre[:, n_ * 512:(n_ + 1) * 512], xres[:, n_ * 512:(n_ + 1) * 512], ALPHA, pmix[n_], ALU.mult, ALU.add)
        for n_ in range(2):
            k.op("dve", lambda: nc.vector.bn_stats(out=bnst[:, n_, :].ap, in_=pre[:, n_ * 512:(n_ + 1) * 512].ap), [pre], [bnst])
        k.op("dve", lambda: nc.vector.bn_aggr(out=bnag[:, 0:2].ap, in_=bnst[:].ap), [bnst], [bnag])
        k.act(bnag[:, 2:3], bnag[:, 1:2], AF.Sqrt, bias=EPS, scale=1.0)
        k.recip(bnag[:, 2:3], bnag[:, 2:3])
        k.stt("dve", bnag[:, 3:4], bnag[:, 0:1], -1.0, bnag[:, 2:3], ALU.mult, ALU.mult)
        k.ts("dve", tmp[:], pre[:], bnag[:, 2:3], ALU.mult, bnag[:, 3:4], ALU.add)
        k.tt("pool", tmp[:], tmp[:], lnv[:, 0, :], ALU.mult)
        k.tt("dve", pre[:], tmp[:], lnv[:, 1, :], ALU.add)
        k.dma("sp", out_dram[ti], pre[:])

    k.push()
    w_in_s = k.sb("w_in_s", [128, 8, 3072], BF16)
    w_o_s = k.sb("w_o_s", [128, 8, 1024], BF16)
    load_w(w_in_s, w_in, 3072)
    load_w(w_o_s, w_o, 1024)
    wsT = k.sb("wsT", [128, 2, 4, 128], BF16)
    gvA = k.sb("gvA", [128, 4, 512], F32)
    lnA = k.sb("lnA", [128, 2, 1024], F32)
    for i in range(4):
        load_vec(gvA[:, i, :], i, 512)
    for i in range(2):
        load_vec(lnA[:, i, :], 4 + i, 1024)
    k.push()
    wst_f = k.sb("wst_f", [128, 2, 4, 128], F32)
    wsm_f = k.sb("wsm_f", [128, 2, 4, 128], F32)
    k.dma("sp", wst_f[:], wsTl[:].re("a p g t -> p a g t"))
    k.dma("sp", wsm_f[:], wsmask[:].re("a p g t -> p a g t"))
    k.tt("dve", wsT[:], wst_f[:], wsm_f[:], ALU.mult)
    k.pop()
    bs = k.sb("bs", [128, 2, 4], F32)
    k.dma("sp", bs[:], bsl[:].re("a p g -> p a g"))
    dmk = k.sb("dmk", [128, 2, 4, 128], F32)
    k.dma("sp", dmk[:], dmaskT[:].re("a p h t -> p a h t"))
    kdc = k.sb("kdc", [128, 2, 4], F32)
    k.dma("sp", kdc[:], kdec[:].re("a p h -> p a h"))
    xs = [k.sb(f"xs{i}", [128, 1024], F32) for i in range(2)]
    xTs = [k.sb(f"xTs{i}", [128, 8, 128], BF16) for i in range(2)]
    rps = [k.sb(f"rps{i}", [128, 4, 256], F32) for i in range(2)]
    ua_g = k.sb("ua_g", [128, 512], F32)
    va_g = k.sb("va_g", [128, 512], F32)
    va_n = k.sb("va_n", [128, 512], F32)
    va_nb = k.sb("va_nb", [128, 512], BF16)
    tmpA = k.sb("tmpA", [128, 512], F32)
    stA = k.sb("stA", [128, 4, 4], F32)
    rt = [k.sb(f"rt{i}", [128, 4, 64], F32) for i in range(4)]
    qd = k.sb("qd", [128, 4, 128], BF16)
    kk = k.sb("kk", [128, 4, 128], BF16)
    kd = k.sb("kd", [128, 4, 128], BF16)
    vb = k.sb("vb", [128, 4, 128], BF16)
    gs = k.sb("gs", [128, 512], F32)
    qkT = k.sb("qkT", [128, 8, 128], BF16)
    innT = k.sb("innT", [128, 4, 128], BF16)
    S_f = k.sb("S_f", [128, 4, 128], F32)
    S_b = k.sb("S_b", [128, 4, 128], BF16)
    on = k.sb("on", [128, 512], F32)
    cat = k.sb("cat", [128, 1024], BF16)
    catT = k.sb("catT", [128, 8, 128], BF16)
    preA = k.sb("preA", [128, 1024], F32)
    tmpL = k.sb("tmpL", [128, 1024], F32)
    stL = k.sb("stL", [128, 4, 1], F32)
    S0f = k.sb("S0f", [128, 16, 4, 128], F32)
    S0b = [k.sb(f"S0b{i}", [128, 4, 128], BF16) for i in range(2)]
    Zm = k.sb("Zm", [128, 16, 4, 128], BF16)
    kdm = k.sb("kdm", [128, 4, 128], BF16)
    rowm = k.sb("rowm", [128, 16], F32)
    k.dma("sp", S0f[:], sret[:])
    k.memset("pool", Zm[:], 0.0)
    k.memset("dve", S_f[:], 0.0)
    k.memset("dve", S_b[:], 0.0)
    k.red(rowm[:], ident[:].re("p (b e) -> p b e", e=8), ALU.add)
    pT = k.ps("pT", [128, 8, 128], BF16)
    ph = [k.ps(f"ph{i}", [128, 512], F32) for i in range(2)]
    pmi = [k.ps(f"pmi{i}", [128, 512], F32) for i in range(4)]

    def loadA(ti):
        s = ti % 2
        k.dma("sp", xs[s][:], xin[ti])
        k.dma("pool", xTs[s][:], xinT[ti])
        k.dma("sp", rps[s][:], rope[ti])

    def rope_apply(src, cs, sn, dst):
        x1, x2 = src[:, :, 0:64], src[:, :, 64:128]
        k.tt("dve", rt[0][:], x1, cs, ALU.mult)
        k.tt("dve", rt[1][:], x2, sn, ALU.mult)
        k.tt("pool", dst[:, :, 0:64], rt[0][:], rt[1][:], ALU.subtract)
        k.tt("dve", rt[2][:], x1, sn, ALU.mult)
        k.tt("dve", rt[3][:], x2, cs, ALU.mult)
        k.tt("pool", dst[:, :, 64:128], rt[2][:], rt[3][:], ALU.add)

    def computeA(ti):
        s = ti % 2
        sm = 1 if ti == NT - 1 else 0
        xT = xTs[s]
        rp = rps[s]
        for n in range(6):
            p = ph[n % 2][:]
            for kc in range(8):
                k.mm(p, xT[:, kc, :], w_in_s[:, kc, n * 512:(n + 1) * 512], kc == 0, kc == 7)
            p4 = p.re("p (h c) -> p h c", h=4)
            if n == 0:
                k.act(ua_g[:], p, AF.Gelu_apprx_tanh)
            elif n == 1:
                k.act(va_g[:], p, AF.Gelu_apprx_tanh)
            elif n == 2:
                rq = rp[:, 0, :].re("p (h c) -> p h c", h=4)
                rs_ = rp[:, 1, :].re("p (h c) -> p h c", h=4)
                rope_apply(p4, rq, rs_, qd)
            elif n == 3:
                rq = rp[:, 2, :].re("p (h c) -> p h c", h=4)
                rs_ = rp[:, 3, :].re("p (h c) -> p h c", h=4)
                rope_apply(p4, rq, rs_, kk)
                k.tt("dve", kd[:], kk[:], kdc[:, sm, :].uq(2).bc([128, 4, 128]), ALU.mult)
            elif n == 4:
                k.cp("act", vb[:], p4)
            else:
                k.act(gs[:], p, AF.Silu)
        g4 = lambda t_: t_[:].re("p (g c) -> p g c", g=4)
        group_ln(k, g4(va_g), 4, 128, gvA[:, 0, :].re("p (g c) -> p g c", g=4), gvA[:, 1, :].re("p (g c) -> p g c", g=4),
                 g4(va_n), g4(tmpA), stA)
        if sm:
            k.dma("sp", ogv[:], va_n[:])
        k.cp("act", va_nb[:], va_n[:])
        pm = pmi[0][:]
        for g in range(4):
            k.mm(pm[:, g * 128:(g + 1) * 128], wsT[:, sm, g, :], va_nb[:, g * 128:(g + 1) * 128], True, True)
        for g in range(4):
            k.stt("dve", cat[:, g * 128:(g + 1) * 128], pm[:, g * 128:(g + 1) * 128], bs[:, sm, g:g + 1], ua_g[:, g * 128:(g + 1) * 128],
                  ALU.add, ALU.mult)
        for h in range(4):
            k.tr(pT[:, h, :], qd[:, h, :], ident[:])
            k.tr(pT[:, 4 + h, :], kk[:, h, :], ident[:])
        k.cp("act", qkT[:], pT[:])
        pin = pmi[1][:].re("p (h c) -> p h c", h=4)
        for h in range(4):
            k.mm(pin[:, h, :], qkT[:, 4 + h, :], qkT[:, h, :], True, True)
        k.tt("dve", innT[:], pin, dmk[:, sm], ALU.mult)
        po = pmi[2][:].re("p (h c) -> p h c", h=4)
        if sm:
            for b in range(16):
                k.cp("pool", Zm[:, b, :, b * 8:(b + 1) * 8], qkT[:, 0:4, b * 8:(b + 1) * 8])
        if sm:
            k.mm(pmi[2][:], zeros[:, 0:128], zeros[:], True, True)
            for h in range(4):
                k.mm(po[:, h, :], innT[:, h, :], vb[:, h, :], False, False, skip=True)
            for b in range(16):
                k.cp("act", S0b[b % 2][:], S0f[:, b])
                for h in range(4):
                    k.mm(po[:, h, :], Zm[:, b, h, :], S0b[b % 2][:, h, :], False, b == 15, skip=True)
        for h in range(4):
            if sm:
                pass
            elif ti == 0:
                k.mm(po[:, h, :], innT[:, h, :], vb[:, h, :], True, True)
            else:
                k.mm(po[:, h, :], innT[:, h, :], vb[:, h, :], True, False)
                k.mm(po[:, h, :], qkT[:, h, :], S_b[:, h, :], False, True)
        pS = pmi[3][:].re("p (h c) -> p h c", h=4)
        if sm:
            for b in range(16):
                k.ts("dve", kdm[:], kd[:], rowm[:, b:b + 1], ALU.mult)
                for h in range(4):
                    k.mm(pS[:, h, :], kdm[:, h, :], vb[:, h, :], True, True)
                for h in range(4):
                    k.stt("dve", S0f[:, b, h, :], S0f[:, b, h, :], gam[h] ** 8, pS[:, h, :], ALU.mult, ALU.add)
            k.dma("sp", ors[:].re("b h d e -> d b h e"), S0f[:])
        else:
            for h in range(4):
                k.mm(pS[:, h, :], kd[:, h, :], vb[:, h, :], True, True)
            for h in range(4):
                k.stt("dve", S_f[:, h, :], S_f[:, h, :], gam[h] ** 128, pS[:, h, :], ALU.mult, ALU.add)
            if ti == NT - 2:
                k.dma("sp", orp[:].re("h d e -> d h e"), S_f[:])
            else:
                k.cp("act", S_b[:], S_f[:])
        group_ln(k, po, 4, 128, gvA[:, 2, :].re("p (g c) -> p g c", g=4), gvA[:, 3, :].re("p (g c) -> p g c", g=4),
                 g4(on), g4(tmpA), stA)
        k.tt("dve", cat[:, 512:1024], on[:], gs[:], ALU.mult)
        for kc in range(8):
            k.tr(pT[:, kc, :], cat[:, kc * 128:(kc + 1) * 128], ident[:])
        k.cp("act", catT[:], pT[:])
        for n in range(2):
            for kc in range(8):
                k.mm(ph[n][:], catT[:, kc, :], w_o_s[:, kc, n * 512:(n + 1) * 512], kc == 0, kc == 7)
        final_ln(ti, xs[s][:], [ph[0][:], ph[1][:]], lnA, x1d, preA, tmpL, stL)

    AT = int(os.environ.get("MK_AT", str(NT)))
    if AT > 0:
        loadA(0)
    for ti in range(AT):
        if ti + 1 < AT:
            loadA(ti + 1)
        computeA(ti)
    k.pop()
    if STAGE <= 1:
        k.pop()
        k.barrier()
        return nc

    k.push()
    wq_s = k.sb("wq_s", [128, 8, 1024], BF16)
    wk_s = k.sb("wk_s", [128, 8, 1024], BF16)
    wv_s = k.sb("wv_s", [128, 8, 1024], BF16)
    wo_s = k.sb("wo_s", [128, 8, 1024], BF16)
    load_w(wk_s, ca_wk, 1024)
    load_w(wv_s, ca_wv, 1024)
    load_w(wq_s, ca_wq, 1024)
    load_w(wo_s, ca_wo, 1024)
    mT = k.sb("mT", [128, 8, 256], BF16)
    for a_ in range(2):
        k.dma("pool", mT[:, a_ * 4:(a_ + 1) * 4, :], mTl[:, a_ * 4:(a_ + 1) * 4, :])
    kTm = k.sb("kTm", [128, 8, 256], BF16)
    vbm = k.sb("vbm", [128, 2, 1024], BF16)
    kvout = k.sb("kvout", [128, 2, 2, 1024], F32)
    x1s = [k.sb(f"x1s{i}", [128, 1024], F32) for i in range(2)]
    x1b = k.sb("x1b", [128, 1024], BF16)
    x1T = k.sb("x1T", [128, 8, 128], BF16)
    qT = k.sb("qT", [128, 8, 128], BF16)
    Zq = k.sb("Zq", [128, 16, 8, 128], BF16)
    kTb = [k.sb(f"kTb{i}", [128, 8, 256], BF16) for i in range(2)]
    vbs = [k.sb(f"vbs{i}", [128, 2, 1024], BF16) for i in range(2)]
    mx = k.sb("mx", [128, 4], F32)
    rsum = k.sb("rsum", [128, 4], F32)
    pexp = k.sb("pexp", [128, 4, 256], BF16)
    pTs = k.sb("pTs", [128, 8, 128], BF16)
    cab = k.sb("cab", [128, 4, 256], BF16)
    caT = k.sb("caT", [128, 8, 128], BF16)
    preB = k.sb("preB", [128, 1024], F32)
    tmpB = k.sb("tmpB", [128, 1024], F32)
    stB = k.sb("stB", [128, 4, 1], F32)
    lnB = k.sb("lnB", [128, 2, 1024], F32)
    for i in range(2):
        load_vec(lnB[:, i, :], 6 + i, 1024)
    pT = k.ps("pTb", [128, 8, 128], BF16)
    pq = k.ps("pq", [128, 2, 512], F32)
    psc = k.ps("psc", [128, 2, 512], F32)
    ppv = k.ps("ppv", [128, 2, 512], F32)
    for b_ in range(0, 16, 4):
        k.memset("pool", Zq[:, b_:b_ + 4], 0.0)
    BS = int(os.environ.get("MK_BS", "9"))
    for which, wsb, odr in ((0, wk_s, omk), (1, wv_s, omv)) if BS >= 1 else ():
        for mt in range(2):
            for n in range(2):
                p = pq[:, n, :]
                for kc in range(8):
                    k.mm(p, mT[:, kc, mt * 128:(mt + 1) * 128], wsb[:, kc, n * 512:(n + 1) * 512], kc == 0, kc == 7)
                k.cp("act", kvout[:, which, mt, n * 512:(n + 1) * 512], p)
                if which == 1:
                    k.cp("dve", vbm[:, mt, n * 512:(n + 1) * 512], kvout[:, which, mt, n * 512:(n + 1) * 512])
        k.dma("sp", odr[:].re("(mt p) n -> p mt n", p=128), kvout[:, which])
    for g in range(8 if BS >= 2 else 0):
        p = psc[:, g % 2, 0:256]
        for kc in range(8):
            k.mm(p, wk_s[:, kc, g * 128:(g + 1) * 128], mT[:, kc, :], kc == 0, kc == 7)
        k.cp("act", kTm[:, g, :], p)

    def loadB(ti):
        k.dma("sp", x1s[ti % 2][:], x1d[ti])

    def computeB(ti):
        s = ti % 2
        sm = ti == NT - 1
        k.cp("act", x1b[:], x1s[s][:])
        for kc in range(8):
            k.tr(pT[:, kc, :], x1b[:, kc * 128:(kc + 1) * 128], ident[:])
        k.cp("dve", x1T[:], pT[:])
        pq8 = pq[:].re("p a (g t) -> p (a g) t", g=4)
        for g in range(8):
            for kc in range(8):
                k.mm(pq8[:, g, :], wq_s[:, kc, g * 128:(g + 1) * 128], x1T[:, kc, :], kc == 0, kc == 7)
        k.ts("dve", qT[:], pq8, 1.0 / 16.0, ALU.mult)
        sc = psc[:].re("p a (h m) -> p (a h) m", h=2)
        pv = ppv[:].re("p a (h m) -> p (a h) m", h=2)
        if not sm:
            for h in range(4):
                for dc in range(2):
                    k.mm(sc[:, h, :], qT[:, h * 2 + dc, :], kTm[:, h * 2 + dc, :], dc == 0, dc == 1)
        else:
            for a in range(2):
                k.mm(psc[:, a, :], zeros[:, 0:128], zeros[:], True, True)
            for b in range(16):
                k.cp("pool", Zq[:, b, :, b * 8:(b + 1) * 8], qT[:, :, b * 8:(b + 1) * 8])
            def ldk(b_):
                for a_ in range(2):
                    k.dma("pool", kTb[b_ % 2][:, a_ * 4:(a_ + 1) * 4, :], ckT[b_][:, a_ * 4:(a_ + 1) * 4, :])
            ldk(0)
            for b in range(16):
                if b + 1 < 16:
                    ldk(b + 1)
                for g in range(8):
                    k.mm(sc[:, g // 2, :], Zq[:, b, g, :], kTb[b % 2][:, g, :], False, (b == 15 and g % 2 == 1), skip=True)
        k.red(mx[:], sc, ALU.max)
        k.ts("dve", mx[:], mx[:], -1.0, ALU.mult)
        for h in range(4):
            k.act(pexp[:, h, :], sc[:, h, :], AF.Exp, bias=mx[:, h:h + 1], scale=1.0, accum=rsum[:, h:h + 1])
        k.recip(rsum[:], rsum[:])
        for h in range(4):
            for mc in range(2):
                k.tr(pT[:, h * 2 + mc, :], pexp[:, h, mc * 128:(mc + 1) * 128], ident[:])
        k.cp("dve", pTs[:], pT[:])
        if not sm:
            for h in range(4):
                for mc in range(2):
                    k.mm(pv[:, h, :], pTs[:, h * 2 + mc, :], vbm[:, mc, h * 256:(h + 1) * 256], mc == 0, mc == 1)
        else:
            for a in range(2):
                k.mm(ppv[:, a, :], zeros[:, 0:128], zeros[:], True, True)
            for b in range(16):
                k.cp("pool", Zq[:, b, :, b * 8:(b + 1) * 8], pTs[:, :, b * 8:(b + 1) * 8])
            def ldv(b_):
                for a_ in range(2):
                    k.dma("pool", vbs[b_ % 2][:, a_, :], cvl[b_][:, a_, :])
            ldv(0)
            for b in range(16):
                if b + 1 < 16:
                    ldv(b + 1)
                for h in range(4):
                    for mc in range(2):
                        k.mm(pv[:, h, :], Zq[:, b, h * 2 + mc, :], vbs[b % 2][:, mc, h * 256:(h + 1) * 256], False,
                             (b == 15 and mc == 1), skip=True)
        k.tt("dve", cab[:], pv, rsum[:].uq(2).bc([128, 4, 256]), ALU.mult)
        cab2 = cab[:].re("p h m -> p (h m)")
        for kc in range(8):
            k.tr(pT[:, kc, :], cab2[:, kc * 128:(kc + 1) * 128], ident[:])
        k.cp("act", caT[:], pT[:])
        pco = pq[:].re("p a c -> p (a c)")
        for n in range(2):
            for kc in range(8):
                k.mm(pco[:, n * 512:(n + 1) * 512], caT[:, kc, :], wo_s[:, kc, n * 512:(n + 1) * 512], kc == 0, kc == 7)
        final_ln(ti, x1s[s][:], [pco[:, 0:512], pco[:, 512:1024]], lnB, x2d, preB, tmpB, stB)

    BT = int(os.environ.get("MK_BT", str(NT)))
    tilesB = list(range(NT)) if BT >= NT else ([NT - 1] if BT < 0 else list(range(BT)))
    if tilesB:
        loadB(tilesB[0])
    for i_, ti in enumerate(tilesB):
        if i_ + 1 < len(tilesB):
            loadB(tilesB[i_ + 1])
        computeB(ti)
    k.pop()
    if STAGE <= 2:
        k.pop()
        k.barrier()
        return nc

    x2T = k.sb("x2T", [128, 8, NT * 128], BF16)
    tau = k.sb("tau", [128, NT, 8], F32)
    ebias = k.sb("ebias", [128, NT, 8], F32)
    k.push()
    pwq = k.sb("pwq", [128, 8, 2048], BF16)
    load_w(pwq, peer_wq, 2048)
    skT = k.sb("skT", [128, 16, 128], BF16)
    for a_ in range(2):
        k.dma("pool", skT[:, a_ * 8:(a_ + 1) * 8, :], skTl[:, a_ * 8:(a_ + 1) * 8, :])
    x2s = [k.sb(f"x2s{i}", [128, 1024], F32) for i in range(2)]
    x2b = k.sb("x2b", [128, 1024], BF16)
    qpT = k.sb("qpT", [128, 16, 128], BF16)
    s_sbs = [k.sb(f"s_sb{i}", [128, 16, 128], F32) for i in range(2)]
    s_tmp = k.sb("s_tmp", [128, 16, 128], F32)
    sv = k.sb("sv", [128, 16, 16], F32)
    cand = k.sb("cand", [128, 8, 256], F32)
    cand2 = k.sb("cand2", [128, 8, 256], F32)
    fv = k.sb("fv", [128, 8, 16], F32)
    fe = k.sb("fe", [128, 8, 16], F32)
    fv3 = k.sb("fv3", [128, 8, 8], F32)
    zs = k.sb("zs", [128, 8], F32)
    pT = k.ps("pTc", [128, 8, 128], BF16)
    pqp = k.ps("pqp", [128, 4, 512], F32)
    pss = pqp

    def loadC(ti):
        k.dma("sp", x2s[ti % 2][:], x2d[ti])

    def computeC1(ti):
        s = ti % 2
        s_sb = s_sbs[s]
        k.cp("act", x2b[:], x2s[s][:])
        for kc in range(8):
            k.tr(pT[:, kc, :], x2b[:, kc * 128:(kc + 1) * 128], ident[:])
        k.cp("act", x2T[:, :, ti * 128:(ti + 1) * 128], pT[:])
        for hq in range(4):
            for j in range(4):
                hc = hq * 4 + j
                for kc in range(8):
                    k.mm(pqp[:, hq, j * 128:(j + 1) * 128], pwq[:, kc, hc * 128:(hc + 1) * 128], x2T[:, kc, ti * 128:(ti + 1) * 128],
                         kc == 0, kc == 7)
            k.cp("act", qpT[:, hq * 4:(hq + 1) * 4, :], pqp[:, hq, :].re("p (j t) -> p j t", j=4))
        for hc in range(16):
            k.mm(pss[:, hc // 4, (hc % 4) * 128:(hc % 4 + 1) * 128], qpT[:, hc, :], skT[:, hc, :], True, True)
        k.cp("act", s_sb[:], pss[:].re("p a (j t) -> p (a j) t", j=4))
        k.dma("sp", sd[ti].re("p (a t) -> p a t", a=16), s_sb[:])

    def computeC2(ti):
        s_sb = s_sbs[ti % 2]
        for hc in range(16):
            k.max8(sv[:, hc, 0:8], s_sb[:, hc, :])
            k.mrep(s_tmp[:, hc, :], sv[:, hc, 0:8], s_sb[:, hc, :], -1e30)
            k.max8(sv[:, hc, 8:16], s_tmp[:, hc, :])
        sv4 = sv[:].re("p (h c) r -> p h c r", c=2)
        c4 = cand[:].re("p h (a b) -> p h a b", a=16)
        k.tt("dve", c4, sv4[:, :, 0, :].uq(3).bc([128, 8, 16, 16]), sv4[:, :, 1, :].uq(2).bc([128, 8, 16, 16]), ALU.add)
        for h in range(8):
            k.max8(fv[:, h, 0:8], cand[:, h, :])
            k.mrep(cand2[:, h, :], fv[:, h, 0:8], cand[:, h, :], -1e30)
            k.max8(fv[:, h, 8:16], cand2[:, h, :])
            k.mrep(cand2[:, h, :], fv[:, h, 8:16], cand2[:, h, :], -1e30)
            k.max8(fv3[:, h, :], cand2[:, h, :])
        k.tt("dve", tau[:, ti, :], fv[:, :, 15], fv3[:, :, 0], ALU.add)
        k.ts("dve", tau[:, ti, :], tau[:, ti, :], 0.5, ALU.mult)
        k.tt("dve", fe[:], fv[:], fv[:, :, 0:1].bc([128, 8, 16]), ALU.subtract)
        k.act(fe[:], fe[:], AF.Exp)
        k.red(zs[:], fe[:], ALU.add)
        k.act(zs[:], zs[:], AF.Ln)
        k.stt("dve", ebias[:, ti, :], fv[:, :, 0], -1.0, zs[:], ALU.mult, ALU.subtract)

    loadC(0)
    loadC(1)
    computeC1(0)
    for ti in range(NT):
        if ti + 2 < NT:
            loadC(ti + 2)
        if ti + 1 < NT:
            computeC1(ti + 1)
        computeC2(ti)
    k.pop()
    if STAGE <= 3:
        k.pop()
        k.barrier()
        return nc

    groups = [list(range(g, min(g + 3, NT))) for g in range(0, NT, 3)]
    k.push()
    GM = 384
    WT = k.sb("WT", [128, 128, GM], BF16)
    stD = k.sb("stD", [128, 4, 1], F32)
    lnD = k.sb("lnD", [128, 2, 1024], F32)
    for i in range(2):
        load_vec(lnD[:, i, :], 8 + i, 1024)
    NB = 4
    POOLSET = [int(c_) for c_ in os.environ.get("MK_POOLSET", "01010101")]
    for grp in groups:
        n = len(grp)
        G = n * 128
        col0 = grp[0] * 128
        k.push()
        ssl = [k.sb(f"ssl{i}", [128, 16, 128], F32) for i in range(2)]
        s0p = [k.sb(f"s0p{i}", [128, 8, 128], F32) for i in range(2)]
        bias2 = [k.sb(f"bias2{i}", [128, 8], F32) for i in range(2)]
        cb = [k.sb(f"cb{i}", [128, 8, 128], BF16) for i in range(NB)]
        eb = [k.sb(f"eb{i}", [128, 8, 128], BF16) for i in range(NB)]
        wm = [k.sb(f"wm{i}", [128, 8, 128], BF16) for i in range(NB)]
        wsum = [k.sb(f"wsum{i}", [128, 1024], BF16) for i in range(2)]
        pacc = [k.ps(f"pacc{i}", [128, 2, 512], F32) for i in range(2)]
        ptr = [k.ps(f"ptr{i}", [128, 8, 128], BF16) for i in range(2)]
        k.dma("sp", ssl[grp[0] % 2][:], sd[grp[0]].re("p (a t) -> p a t", a=16))
        steps = [(tt, ti, ib, h) for tt, ti in enumerate(grp) for ib in range(16) for h in range(8)]
        NS = len(steps)

        def stA(kk):
            tt, ti, ib, h = steps[kk]
            if ib == 0 and h == 0:
                if tt + 1 < n:
                    k.dma("sp", ssl[grp[tt + 1] % 2][:], sd[grp[tt + 1]].re("p (a t) -> p a t", a=16))
                s4_ = ssl[ti % 2][:].re("p (h c) t -> p h c t", c=2)
                k.tt("dve", s0p[ti % 2][:], s4_[:, :, 0, :], tau[:, ti, :].uq(2).bc([128, 8, 128]), ALU.subtract)
                k.tt("dve", bias2[ti % 2][:], tau[:, ti, :], ebias[:, ti, :], ALU.add)
            s4_ = ssl[ti % 2][:].re("p (h c) t -> p h c t", c=2)
            k.tt("pool" if POOLSET[kk % 8] else "dve", cb[kk % NB][:], s0p[ti % 2][:, h, ib * 8:(ib + 1) * 8].uq(2).bc([128, 8, 128]),
                 s4_[:, h, 1, :].uq(1).bc([128, 8, 128]), ALU.add)

        BIGNEG = 1.0e9

        def stB1(kk):
            if kk % 2 == 0:
                k.act(eb[kk % NB][:], cb[kk % NB][:], AF.Prelu, alpha=BIGNEG)
            else:
                k.stt("dve", eb[kk % NB][:], cb[kk % NB][:], BIGNEG, cb[kk % NB][:], ALU.mult, ALU.min)

        def stB(kk):
            tt, ti, ib, h = steps[kk]
            k.act(wm[kk % NB][:], eb[kk % NB][:], AF.Exp, bias=bias2[ti % 2][:, h:h + 1], scale=1.0)

        def stC(kk):
            tt, ti, ib, h = steps[kk]
            blk = kk // 8
            acc = pacc[blk % 2]
            w2 = wm[kk % NB][:].re("p i j -> p (i j)")
            for hf in range(2):
                k.mm(acc[:, hf, :], ident[:], w2[:, hf * 512:(hf + 1) * 512], h == 0, h == 7)

        def post(blk):
            tt, ti, ib, h = steps[blk * 8]
            ws = wsum[blk % 2]
            k.cp("dve", ws[:], pacc[blk % 2][:].re("p a c -> p (a c)"))
            pt_ = ptr[blk % 2]
            for i in range(8):
                k.tr(pt_[:, i, :], ws[:, i * 128:(i + 1) * 128], ident[:])
            k.cp("act", WT[:, ib * 8:(ib + 1) * 8, tt * 128:(tt + 1) * 128], pt_[:])

        pend = []
        for kk in range(NS + 7):
            if kk < NS:
                stA(kk)
            if 0 <= kk - 1 < NS:
                stB1(kk - 1)
            if 0 <= kk - 2 < NS:
                stB(kk - 2)
            if 0 <= kk - 3 < NS:
                stC(kk - 3)
                if (kk - 3) % 8 == 7:
                    pend.append(((kk - 3) // 8, kk + 3))
            while pend and pend[0][1] <= kk:
                post(pend.pop(0)[0])
        assert not pend
        k.pop()
        k.push()
        UTs = [k.sb(f"UTs{i}", [128, 8, 128], BF16) for i in range(3)]
        Vs = [k.sb(f"Vs{i}", [128, 1024], BF16) for i in range(3)]
        ga = [k.sb(f"ga{i}", [128, GM], BF16) for i in range(2)]
        wa = [k.sb(f"wa{i}", [128, GM], BF16) for i in range(2)]
        x2r = k.sb("x2r", [128, 1024], F32)
        preD = k.sb("preD", [128, 1024], F32)
        tmpD = k.sb("tmpD", [128, 1024], F32)
        pA = [k.ps(f"pA{i}", [128, 512], F32) for i in range(2)]
        pf = k.ps("pf", [128, 3, 1024], F32)

        def loadD(i):
            k.dma("pool", UTs[i % 3][:], UTl[i])
            k.dma("pool", Vs[i % 3][:], peer_v[i])

        def mm1(i):
            for kc in range(8):
                k.mm(pA[i % 2][:, 0:G], UTs[i % 3][:, kc, :], x2T[:, kc, col0:col0 + G], kc == 0, kc == 7)

        loadD(0)
        loadD(1)
        mm1(0)
        for i in range(128):
            if i + 2 < 128:
                loadD(i + 2)
            k.act(ga[i % 2][:, 0:G], pA[i % 2][:, 0:G], AF.Gelu_apprx_tanh)
            k.tt("dve", wa[i % 2][:, 0:G], ga[i % 2][:, 0:G], WT[:, i, 0:G], ALU.mult)
            if i + 1 < 128:
                mm1(i + 1)
            for tt in range(n):
                for hf in range(2):
                    k.mm(pf[:, tt, hf * 512:(hf + 1) * 512], wa[i % 2][:, tt * 128:(tt + 1) * 128], Vs[i % 3][:, hf * 512:(hf + 1) * 512],
                         i == 0, i == 127)
        for tt, ti in enumerate(grp):
            k.dma("sp", x2r[:], x2d[ti])
            final_ln(ti, x2r[:], [pf[:, tt, 0:512], pf[:, tt, 512:1024]], lnD, y, preD, tmpD, stD)
        k.pop()
    k.pop()
    k.pop()
    k.barrier()
    return nc


def _kc_layout(w):
    K_, N = w.shape
    return np.ascontiguousarray(w.reshape(K_ // 128, 128, N).transpose(1, 0, 2))


def _consts():
    H = 4
    gam = 1.0 - 2.0 ** (-5.0 - np.arange(H, dtype=np.float64))
    half = 64
    inv = 1.0 / (10000.0 ** (np.arange(half, dtype=np.float64) / half))
    rope = np.zeros((NT, 128, 4, 256), np.float64)
    dmaskT = np.zeros((2, 128, 4, 128), np.float64)
    kdec = np.zeros((2, 128, 4), np.float64)
    for ti in range(NT):
        if ti < NT - 1:
            pos = ti * 128 + np.arange(128)
            loc = np.arange(128)
        else:
            loc = np.arange(128) % 8
            pos = 16384 + loc
        ang = np.float32(pos.astype(np.float32)[:, None] * inv.astype(np.float32)[None, :]).astype(np.float64)
        c, s = np.cos(ang), np.sin(ang)
        qdec = gam[None, :] ** (loc[:, None] + 1.0)
        rope[ti, :, 0] = (c[:, None, :] * qdec[:, :, None]).reshape(128, 256)
        rope[ti, :, 1] = (s[:, None, :] * qdec[:, :, None]).reshape(128, 256)
        rope[ti, :, 2] = np.tile(c * 128 ** -0.5, (1, 4))
        rope[ti, :, 3] = np.tile(s * 128 ** -0.5, (1, 4))
    j = np.arange(128)
    for h in range(H):
        m = (j[None, :] >= j[:, None]).astype(np.float64)
        dmaskT[0, :, h, :] = m * gam[h] ** (-(j[:, None] + 1.0))
        jl, bl = j % 8, j // 8
        ms = ((bl[None, :] == bl[:, None]) & (jl[None, :] >= jl[:, None])).astype(np.float64)
        dmaskT[1, :, h, :] = ms * gam[h] ** (-(jl[:, None] + 1.0))
        kdec[0, :, h] = gam[h] ** (127.0 - j)
        kdec[1, :, h] = gam[h] ** (7.0 - jl)
    wsmask = np.zeros((2, 128, 4, 128), np.float32)
    wsmask[0] = (j[None, :] >= j[:, None]).astype(np.float32)[:, None, :]
    wsmask[1] = wsmask[0]
    return rope.astype(np.float32), dmaskT.astype(np.float32), kdec.astype(np.float32), wsmask


_NC_CACHE = {}


def kernel(**inp):
    f = lambda a: np.ascontiguousarray(np.asarray(a, dtype=np.float32))
    rope, dmaskT, kdec, wsmask = _consts()
    ident = np.eye(128, dtype=np.float32).astype(ml_dtypes.bfloat16)
    w_s = f(inp["w_s"])[0]
    b_s = f(inp["b_s"])[0]
    wsTl = np.zeros((2, 128, 4, 128), np.float32)
    wsTl[0] = w_s.transpose(2, 0, 1)
    bsl = np.zeros((2, 128, 4), np.float32)
    bsl[0] = b_s.T
    for b in range(16):
        wsTl[1, b * 8:(b + 1) * 8, :, b * 8:(b + 1) * 8] = w_s[:, :8, :8].transpose(2, 0, 1)
        bsl[1, b * 8:(b + 1) * 8, :] = b_s[:, :8].T
    vecs = np.zeros((10, 1024), np.float32)
    vecs[0, :512] = f(inp["gate_ln_g"])[0].reshape(-1)
    vecs[1, :512] = f(inp["gate_ln_b"])[0].reshape(-1)
    vecs[2, :512] = f(inp["ret_gn_g"])[0].reshape(-1)
    vecs[3, :512] = f(inp["ret_gn_b"])[0].reshape(-1)
    for i, nm in enumerate(["ln1_g", "ln1_b", "ln2_g", "ln2_b", "ln3_g", "ln3_b"]):
        vecs[4 + i] = f(inp[nm])[0]
    shared = {
        "w_in": _kc_layout(f(inp["w_in"])[0]), "w_o": _kc_layout(f(inp["w_o"])[0]),
        "ca_wq": _kc_layout(f(inp["ca_wq"])[0]), "ca_wk": _kc_layout(f(inp["ca_wk"])[0]),
        "ca_wv": _kc_layout(f(inp["ca_wv"])[0]), "ca_wo": _kc_layout(f(inp["ca_wo"])[0]),
        "peer_wq": _kc_layout(f(inp["peer_wq"])[0]),
        "skTl": np.ascontiguousarray(f(inp["peer_subkeys"])[0].reshape(16, 128, 128).transpose(2, 0, 1)),
        "UTl": np.ascontiguousarray(f(inp["peer_u"])[0].reshape(128, 128, 8, 128).transpose(0, 3, 2, 1)),
        "peer_v": f(inp["peer_v"])[0].reshape(128, 128, 1024),
        "wsTl": wsTl, "wsmask": wsmask, "bsl": bsl, "vecs": vecs, "rope": rope, "dmaskT": dmaskT, "kdec": kdec, "ident": ident,
    }
    if STAGE < 4:
        shared["UTl"] = shared["UTl"][:1]
        shared["peer_v"] = shared["peer_v"][:1]
    xp, xsm = f(inp["x_prompt"]), f(inp["x_sample"])
    memp = f(inp["mem_prompt"])
    cmk, cmv, sr = f(inp["cache_mem_k"])[0], f(inp["cache_mem_v"])[0], f(inp["state_ret"])[0]
    in_maps = []
    for c in range(8):
        xin = np.concatenate([xp[c].reshape(16, 128, 1024), xsm[16 * c:16 * c + 16].reshape(1, 128, 1024)], axis=0)
        m = dict(shared)
        m["xin"] = np.ascontiguousarray(xin)
        m["xinT"] = np.ascontiguousarray(xin.reshape(NT, 128, 8, 128).transpose(0, 3, 2, 1))
        m["mTl"] = np.ascontiguousarray(memp[c].reshape(256, 8, 128).transpose(2, 1, 0))
        m["ckT"] = np.ascontiguousarray(cmk[16 * c:16 * c + 16].reshape(16, 256, 8, 128).transpose(0, 3, 2, 1))
        m["cvl"] = np.ascontiguousarray(cmv[16 * c:16 * c + 16].reshape(16, 2, 128, 1024).transpose(0, 2, 1, 3))
        m["sret"] = np.ascontiguousarray(sr[16 * c:16 * c + 16].transpose(2, 0, 1, 3))
        in_maps.append(m)
    if "nc" not in _NC_CACHE:
        _NC_CACHE["nc"] = build()
    nc = _NC_CACHE["nc"]
    res = run_bass_kernel_spmd(nc, in_maps, core_ids=list(range(8)))
    R = res.results
    _NC_CACHE["last"] = R
    g = lambda name: [np.asarray(r[name], dtype=np.float32) for r in R]
    ys = g("y")
    y_prompt = np.stack([a[:16].reshape(2048, 1024) for a in ys])
    y_sample = np.concatenate([a[16].reshape(16, 8, 1024) for a in ys], axis=0)
    new_mem_k = np.stack([a.reshape(256, 4, 256) for a in g("omk")])[None]
    new_mem_v = np.stack([a.reshape(256, 4, 256) for a in g("omv")])[None]
    new_ret_prompt = np.stack(g("orp"))[None]
    new_ret_sample = np.concatenate(g("ors"), axis=0)[None]
    new_gate_v = np.concatenate([a.reshape(16, 8, 4, 128) for a in g("ogv")], axis=0)[None]
    return (y_prompt, y_sample, new_mem_k, new_mem_v, new_ret_prompt, new_ret_sample, new_gate_v)
```

```python
import os
import math
from contextlib import ExitStack
import numpy as np
import ml_dtypes
import concourse.bass as bass
import concourse.mybir as mybir
from concourse.bass_utils import run_bass_kernel_spmd

F32 = mybir.dt.float32
BF16 = mybir.dt.bfloat16
AF = mybir.ActivationFunctionType
ALU = mybir.AluOpType
AX = mybir.AxisListType

NT = 17
ALPHA = 2.0 ** 0.25
EPS = 1e-5
STAGE = int(os.environ.get("MK_STAGE", "9"))


class View:
    __slots__ = ("buf", "ap")

    def __init__(self, buf, ap):
        self.buf = buf
        self.ap = ap

    def __getitem__(self, idx):
        return View(self.buf, self.ap[idx])

    def bc(self, shape):
        return View(self.buf, self.ap.broadcast_to(list(shape)))

    def uq(self, axis):
        return View(self.buf, self.ap.unsqueeze(axis))

    def re(self, pat, **kw):
        return View(self.buf, self.ap.rearrange(pat, **kw))


class Buf:
    __slots__ = ("t", "name", "lw", "rd", "dsem", "dcnt", "psum", "is_dram", "wtoks")

    def __init__(self, t, name, psum=False, is_dram=False):
        self.t = t
        self.name = name
        self.psum = psum
        self.is_dram = is_dram
        self.wtoks = []
        self.lw = None
        self.rd = []
        self.dsem = None
        self.dcnt = 0

    def __getitem__(self, idx):
        return View(self, self.t[idx])


class K:
    SEM_LIMIT = 30000

    def __init__(self, nc):
        self.nc = nc
        self.eng = {"pe": nc.tensor, "act": nc.scalar, "dve": nc.vector, "pool": nc.gpsimd, "sp": nc.sync}
        self.esem, self.ecnt = {}, {}
        self.known = {e: {} for e in self.eng}
        self.nsem = 0
        self.allsems = []
        for e in self.eng:
            self.esem[e] = self._newsem("e_" + e)
            self.ecnt[e] = 0
        self.dbufs = []
        self.stacks = []
        self.nalloc = 0
        self.scope_bufs = []
        self.sem_by_name = {}

    def _newsem(self, name):
        self.nsem += 1
        s = self.nc.alloc_semaphore(name=f"{name}_{self.nsem}")
        self.allsems.append(s)
        return s

    def push(self):
        self.stacks.append(ExitStack())
        self.scope_bufs.append([])

    def pop(self):
        self.barrier()
        self.stacks.pop().close()
        for b in self.scope_bufs.pop():
            if b.dsem is not None:
                self.sem_by_name[b.name.rsplit("_s", 1)[0]] = (b.dsem, b.dcnt)
                self.dbufs.remove(b)
                b.dsem = None

    def sb(self, name, shape, dt=F32):
        self.nalloc += 1
        name = f"{name}_s{self.nalloc}"
        t = self.stacks[-1].enter_context(self.nc.sbuf_tensor(name, list(shape), dt))
        b = Buf(t, name)
        self.scope_bufs[-1].append(b)
        return b

    def ps(self, name, shape, dt=F32):
        self.nalloc += 1
        name = f"{name}_p{self.nalloc}"
        t = self.stacks[-1].enter_context(self.nc.psum_tensor(name, list(shape), dt))
        return Buf(t, name, psum=True)

    def dram(self, name, shape, dt=F32, kind="Internal"):
        return Buf(self.nc.dram_tensor(name, list(shape), dt, kind=kind).ap(), name, is_dram=True)

    def _wait(self, e, deps):
        kn = self.known[e]
        best = {}
        for (sem, val, _) in deps:
            key = id(sem)
            if kn.get(key, 0) >= val:
                continue
            if key not in best or best[key][1] < val:
                best[key] = (sem, val)
        for key, (sem, val) in best.items():
            self.eng[e].wait_ge(sem, val)
            kn[key] = val

    def op(self, e, fn, reads=(), writes=(), pe_accum=False):
        deps = []
        for b in reads:
            if b.lw is not None:
                deps.append(b.lw)
            if b.psum:
                deps.extend(t_ for t_ in b.rd if t_[2] != e)
        for b in writes:
            if b.lw is not None and not (pe_accum and b.lw[2] == "pe"):
                deps.append(b.lw)
            deps.extend(b.rd)
        self._wait(e, deps)
        if self.ecnt[e] >= self.SEM_LIMIT:
            self.esem[e] = self._newsem("e_" + e)
            self.ecnt[e] = 0
        ins = fn()
        self.ecnt[e] += 1
        ins.then_inc(self.esem[e], 1)
        tok = (self.esem[e], self.ecnt[e], e)
        for b in reads:
            if len(b.rd) > 64:
                b.rd = b.rd[-48:]
            b.rd.append(tok)
        for b in writes:
            b.lw = tok
            b.rd = []
        return ins

    def dma(self, q, out, in_, **kw):
        src, dst = in_.buf, out.buf
        deps = []
        if src.lw is not None:
            deps.append(src.lw)
        deps.extend(src.wtoks)
        if dst.lw is not None and dst.lw[2] != "dma":
            deps.append(dst.lw)
        deps.extend(dst.rd)
        self._wait(q, deps)
        own = src if (dst.is_dram and not src.is_dram) else dst
        if own.dsem is None:
            base = own.name.rsplit("_s", 1)[0]
            if base in self.sem_by_name:
                own.dsem, own.dcnt = self.sem_by_name.pop(base)
            else:
                own.dsem = self._newsem("d_" + own.name)
            self.dbufs.append(own)
        ins = self.eng[q].dma_start(out=out.ap, in_=in_.ap, **kw)
        own.dcnt += 16
        ins.then_inc(own.dsem, 16)
        tok = (own.dsem, own.dcnt, "dma")
        src.rd.append(tok)
        if own is src:
            dst.wtoks.append(tok)
            if len(dst.wtoks) > 40:
                dst.wtoks = dst.wtoks[-40:]
        else:
            dst.lw = tok
            dst.rd = []
        return ins

    def barrier(self, engines=None):
        deps = [(self.esem[e], self.ecnt[e], e) for e in self.eng if self.ecnt[e] > 0]
        deps += [(b.dsem, b.dcnt, "dma") for b in self.dbufs if b.dcnt > 0]
        for e in (engines or self.eng):
            self._wait(e, deps)

    @staticmethod
    def _bufs(*xs):
        return [x.buf for x in xs if isinstance(x, View)]

    @staticmethod
    def _a(x):
        return x.ap if isinstance(x, View) else x

    def tt(self, e, out, a, b, op):
        E = self.eng[e]
        return self.op(e, lambda: E.tensor_tensor(out=out.ap, in0=a.ap, in1=b.ap, op=op), self._bufs(a, b), [out.buf])

    def ts(self, e, out, a, s1, op0, s2=None, op1=None):
        E = self.eng[e]
        kw = {} if op1 is None else {"op1": op1}
        return self.op(e, lambda: E.tensor_scalar(out=out.ap, in0=a.ap, scalar1=self._a(s1), scalar2=self._a(s2), op0=op0, **kw),
                       self._bufs(a, s1, s2), [out.buf])

    def stt(self, e, out, a, s, b, op0, op1):
        E = self.eng[e]
        return self.op(e, lambda: E.scalar_tensor_tensor(out=out.ap, in0=a.ap, scalar=self._a(s), in1=b.ap, op0=op0, op1=op1),
                       self._bufs(a, s, b), [out.buf])

    def cp(self, e, out, a):
        E = self.eng[e]
        if e == "act":
            return self.op(e, lambda: E.copy(out=out.ap, in_=a.ap), [a.buf], [out.buf])
        return self.op(e, lambda: E.tensor_copy(out=out.ap, in_=a.ap), [a.buf], [out.buf])

    def act(self, out, a, func, bias=None, scale=None, accum=None, alpha=None):
        kw = {}
        if alpha is not None:
            kw["alpha"] = alpha
        if bias is not None:
            kw["bias"] = self._a(bias)
        if scale is not None:
            kw["scale"] = self._a(scale)
        w = [out.buf]
        if accum is not None:
            kw["accum_out"] = accum.ap
            w.append(accum.buf)
        return self.op("act", lambda: self.nc.scalar.activation(out=out.ap, in_=a.ap, func=func, **kw),
                       self._bufs(a, bias, scale), w)

    def red(self, out, a, op):
        return self.op("dve", lambda: self.nc.vector.tensor_reduce(out=out.ap, in_=a.ap, axis=AX.X, op=op), [a.buf], [out.buf])

    def recip(self, out, a):
        return self.op("dve", lambda: self.nc.vector.reciprocal(out=out.ap, in_=a.ap), [a.buf], [out.buf])

    def memset(self, e, out, val):
        E = self.eng[e]
        return self.op(e, lambda: E.memset(out.ap, val), [], [out.buf])

    def mm(self, out, lhsT, rhs, start, stop, skip=False):
        return self.op("pe", lambda: self.nc.tensor.matmul(out.ap, lhsT=lhsT.ap, rhs=rhs.ap, start=start, stop=stop, skip_group_check=skip),
                       [lhsT.buf, rhs.buf], [out.buf], pe_accum=True)

    def tr(self, out, a, ident):
        return self.op("pe", lambda: self.nc.tensor.transpose(out=out.ap, in_=a.ap, identity=ident.ap),
                       [a.buf, ident.buf], [out.buf], pe_accum=True)

    def max8(self, out, a):
        return self.op("dve", lambda: self.nc.vector.max(out=out.ap, in_=a.ap), [a.buf], [out.buf])

    def mrep(self, out, rep, a, imm):
        return self.op("dve", lambda: self.nc.vector.match_replace(out=out.ap, in_to_replace=rep.ap, in_values=a.ap, imm_value=imm),
                       [rep.buf, a.buf], [out.buf])


def group_ln(k, src, G, C, gt, bt, out, tmp, st):
    s1, nm, s2, rs = st[:, 0, :], st[:, 1, :], st[:, 2, :], st[:, 3, :]
    k.red(s1, src, ALU.add)
    k.ts("dve", nm, s1, -1.0 / C, ALU.mult)
    k.tt("dve", tmp, src, nm.uq(2).bc([128, G, C]), ALU.add)
    k.tt("pool", out, tmp, tmp, ALU.mult)
    k.red(s2, out, ALU.add)
    k.act(rs, s2, AF.Sqrt, bias=EPS, scale=1.0 / C)
    k.recip(rs, rs)
    k.tt("dve", tmp, tmp, rs.uq(2).bc([128, G, C]), ALU.mult)
    k.tt("pool", tmp, tmp, gt, ALU.mult)
    k.tt("dve", out, tmp, bt, ALU.add)


def build():
    nc = bass.Bass("TRN2", target_bir_lowering=False)
    k = K(nc)
    D = k.dram
    EI, EO = "ExternalInput", "ExternalOutput"
    xin = D("xin", [NT, 128, 1024], F32, EI)
    xinT = D("xinT", [NT, 128, 8, 128], F32, EI)
    mTl = D("mTl", [128, 8, 256], F32, EI)
    ckT = D("ckT", [16, 128, 8, 256], F32, EI)
    cvl = D("cvl", [16, 128, 2, 1024], F32, EI)
    sret = D("sret", [128, 16, 4, 128], F32, EI)
    w_in = D("w_in", [128, 8, 3072], F32, EI)
    w_o = D("w_o", [128, 8, 1024], F32, EI)
    ca_wq = D("ca_wq", [128, 8, 1024], F32, EI)
    ca_wk = D("ca_wk", [128, 8, 1024], F32, EI)
    ca_wv = D("ca_wv", [128, 8, 1024], F32, EI)
    ca_wo = D("ca_wo", [128, 8, 1024], F32, EI)
    peer_wq = D("peer_wq", [128, 8, 2048], F32, EI)
    skTl = D("skTl", [128, 16, 128], F32, EI)
    NCH = 128 if STAGE >= 4 else 1
    UTl = D("UTl", [NCH, 128, 8, 128], F32, EI)
    peer_v = D("peer_v", [NCH, 128, 1024], F32, EI)
    wsTl = D("wsTl", [2, 128, 4, 128], F32, EI)
    wsmask = D("wsmask", [2, 128, 4, 128], F32, EI)
    bsl = D("bsl", [2, 128, 4], F32, EI)
    vecs = D("vecs", [10, 1024], F32, EI)
    rope = D("rope", [NT, 128, 4, 256], F32, EI)
    dmaskT = D("dmaskT", [2, 128, 4, 128], F32, EI)
    kdec = D("kdec", [2, 128, 4], F32, EI)
    ident_d = D("ident", [128, 128], BF16, EI)
    y = D("y", [NT, 128, 1024], F32, EO)
    omk = D("omk", [256, 1024], F32, EO)
    omv = D("omv", [256, 1024], F32, EO)
    orp = D("orp", [4, 128, 128], F32, EO)
    ors = D("ors", [16, 4, 128, 128], F32, EO)
    ogv = D("ogv", [128, 512], F32, EO)
    DBG = EO if STAGE < 9 else "Internal"
    x1d = D("x1d", [NT, 128, 1024], F32, DBG)
    x2d = D("x2d", [NT, 128, 1024], F32, DBG)
    sd = D("sd", [NT, 128, 2048], F32)
    outs = [y, omk, omv, orp, ors, ogv]

    gam = [1.0 - 2.0 ** (-5.0 - h) for h in range(4)]

    k.push()
    ident = k.sb("ident", [128, 128], BF16)
    k.dma("sp", ident[:], ident_d[:])
    zeros = k.sb("zeros", [128, 512], BF16)
    k.memset("pool", zeros[:], 0.0)

    bnst = k.sb("bnst", [128, 2, 6], F32)
    bnag = k.sb("bnag", [128, 4], F32)

    def load_vec(dst, i, n):
        k.dma("sp", dst, View(vecs, vecs.t[i, 0:n].partition_broadcast(128)))

    def load_w(dst, src, ncols):
        for kc in range(8):
            for n0 in range(0, ncols, 1024):
                k.dma("pool", dst[:, kc, n0:n0 + 1024], src[:, kc, n0:n0 + 1024])

    def final_ln(ti, xres, pmix, lnv, out_dram, pre, tmp, st):
        for n_ in range(2):
            k.stt("dve", pre[:, n_ * 512:(n_ + 1) * 512], xres[:, n_ * 512:(n_ + 1) * 512], ALPHA, pmix[n_], ALU.mult, ALU.add)
        for n_ in range(2):
            k.op("dve", lambda: nc.vector.bn_stats(out=bnst[:, n_, :].ap, in_=pre[:, n_ * 512:(n_ + 1) * 512].ap), [pre], [bnst])
        k.op("dve", lambda: nc.vector.bn_aggr(out=bnag[:, 0:2].ap, in_=bnst[:].ap), [bnst], [bnag])
        k.act(bnag[:, 2:3], bnag[:, 1:2], AF.Sqrt, bias=EPS, scale=1.0)
        k.recip(bnag[:, 2:3], bnag[:, 2:3])
        k.stt("dve", bnag[:, 3:4], bnag[:, 0:1], -1.0, bnag[:, 2:3], ALU.mult, ALU.mult)
        k.ts("dve", tmp[:], pre[:], bnag[:, 2:3], ALU.mult, bnag[:, 3:4], ALU.add)
        k.tt("pool", tmp[:], tmp[:], lnv[:, 0, :], ALU.mult)
        k.tt("dve", pre[:], tmp[:], lnv[:, 1, :], ALU.add)
        k.dma("sp", out_dram[ti], pre[:])

    k.push()
    w_in_s = k.sb("w_in_s", [128, 8, 3072], BF16)
    w_o_s = k.sb("w_o_s", [128, 8, 1024], BF16)
    load_w(w_in_s, w_in, 3072)
    load_w(w_o_s, w_o, 1024)
    wsT = k.sb("wsT", [128, 2, 4, 128], BF16)
    gvA = k.sb("gvA", [128, 4, 512], F32)
    lnA = k.sb("lnA", [128, 2, 1024], F32)
    for i in range(4):
        load_vec(gvA[:, i, :], i, 512)
    for i in range(2):
        load_vec(lnA[:, i, :], 4 + i, 1024)
    k.push()
    wst_f = k.sb("wst_f", [128, 2, 4, 128], F32)
    wsm_f = k.sb("wsm_f", [128, 2, 4, 128], F32)
    k.dma("sp", wst_f[:], wsTl[:].re("a p g t -> p a g t"))
    k.dma("sp", wsm_f[:], wsmask[:].re("a p g t -> p a g t"))
    k.tt("dve", wsT[:], wst_f[:], wsm_f[:], ALU.mult)
    k.pop()
    bs = k.sb("bs", [128, 2, 4], F32)
    k.dma("sp", bs[:], bsl[:].re("a p g -> p a g"))
    dmk = k.sb("dmk", [128, 2, 4, 128], F32)
    k.dma("sp", dmk[:], dmaskT[:].re("a p h t -> p a h t"))
    kdc = k.sb("kdc", [128, 2, 4], F32)
    k.dma("sp", kdc[:], kdec[:].re("a p h -> p a h"))
    xs = [k.sb(f"xs{i}", [128, 1024], F32) for i in range(2)]
    xTs = [k.sb(f"xTs{i}", [128, 8, 128], BF16) for i in range(2)]
    rps = [k.sb(f"rps{i}", [128, 4, 256], F32) for i in range(2)]
    ua_g = k.sb("ua_g", [128, 512], F32)
    va_g = k.sb("va_g", [128, 512], F32)
    va_n = k.sb("va_n", [128, 512], F32)
    va_nb = k.sb("va_nb", [128, 512], BF16)
    tmpA = k.sb("tmpA", [128, 512], F32)
    stA = k.sb("stA", [128, 4, 4], F32)
    rt = [k.sb(f"rt{i}", [128, 4, 64], F32) for i in range(4)]
    qd = k.sb("qd", [128, 4, 128], BF16)
    kk = k.sb("kk", [128, 4, 128], BF16)
    kd = k.sb("kd", [128, 4, 128], BF16)
    vb = k.sb("vb", [128, 4, 128], BF16)
    gs = k.sb("gs", [128, 512], F32)
    qkT = k.sb("qkT", [128, 8, 128], BF16)
    innT = k.sb("innT", [128, 4, 128], BF16)
    S_f = k.sb("S_f", [128, 4, 128], F32)
    S_b = k.sb("S_b", [128, 4, 128], BF16)
    on = k.sb("on", [128, 512], F32)
    cat = k.sb("cat", [128, 1024], BF16)
    catT = k.sb("catT", [128, 8, 128], BF16)
    preA = k.sb("preA", [128, 1024], F32)
    tmpL = k.sb("tmpL", [128, 1024], F32)
    stL = k.sb("stL", [128, 4, 1], F32)
    S0f = k.sb("S0f", [128, 16, 4, 128], F32)
    S0b = [k.sb(f"S0b{i}", [128, 4, 128], BF16) for i in range(2)]
    Zm = k.sb("Zm", [128, 16, 4, 128], BF16)
    kdm = k.sb("kdm", [128, 4, 128], BF16)
    rowm = k.sb("rowm", [128, 16], F32)
    k.dma("sp", S0f[:], sret[:])
    k.memset("pool", Zm[:], 0.0)
    k.memset("dve", S_f[:], 0.0)
    k.memset("dve", S_b[:], 0.0)
    k.red(rowm[:], ident[:].re("p (b e) -> p b e", e=8), ALU.add)
    pT = k.ps("pT", [128, 8, 128], BF16)
    ph = [k.ps(f"ph{i}", [128, 512], F32) for i in range(2)]
    pmi = [k.ps(f"pmi{i}", [128, 512], F32) for i in range(4)]

    def loadA(ti):
        s = ti % 2
        k.dma("sp", xs[s][:], xin[ti])
        k.dma("pool", xTs[s][:], xinT[ti])
        k.dma("sp", rps[s][:], rope[ti])

    def rope_apply(src, cs, sn, dst):
        x1, x2 = src[:, :, 0:64], src[:, :, 64:128]
        k.tt("dve", rt[0][:], x1, cs, ALU.mult)
        k.tt("dve", rt[1][:], x2, sn, ALU.mult)
        k.tt("pool", dst[:, :, 0:64], rt[0][:], rt[1][:], ALU.subtract)
        k.tt("dve", rt[2][:], x1, sn, ALU.mult)
        k.tt("dve", rt[3][:], x2, cs, ALU.mult)
        k.tt("pool", dst[:, :, 64:128], rt[2][:], rt[3][:], ALU.add)

    def computeA(ti):
        s = ti % 2
        sm = 1 if ti == NT - 1 else 0
        xT = xTs[s]
        rp = rps[s]
        for n in range(6):
            p = ph[n % 2][:]
            for kc in range(8):
                k.mm(p, xT[:, kc, :], w_in_s[:, kc, n * 512:(n + 1) * 512], kc == 0, kc == 7)
            p4 = p.re("p (h c) -> p h c", h=4)
            if n == 0:
                k.act(ua_g[:], p, AF.Gelu_apprx_tanh)
            elif n == 1:
                k.act(va_g[:], p, AF.Gelu_apprx_tanh)
            elif n == 2:
                rq = rp[:, 0, :].re("p (h c) -> p h c", h=4)
                rs_ = rp[:, 1, :].re("p (h c) -> p h c", h=4)
                rope_apply(p4, rq, rs_, qd)
            elif n == 3:
                rq = rp[:, 2, :].re("p (h c) -> p h c", h=4)
                rs_ = rp[:, 3, :].re("p (h c) -> p h c", h=4)
                rope_apply(p4, rq, rs_, kk)
                k.tt("dve", kd[:], kk[:], kdc[:, sm, :].uq(2).bc([128, 4, 128]), ALU.mult)
            elif n == 4:
                k.cp("act", vb[:], p4)
            else:
                k.act(gs[:], p, AF.Silu)
        g4 = lambda t_: t_[:].re("p (g c) -> p g c", g=4)
        group_ln(k, g4(va_g), 4, 128, gvA[:, 0, :].re("p (g c) -> p g c", g=4), gvA[:, 1, :].re("p (g c) -> p g c", g=4),
                 g4(va_n), g4(tmpA), stA)
        if sm:
            k.dma("sp", ogv[:], va_n[:])
        k.cp("act", va_nb[:], va_n[:])
        pm = pmi[0][:]
        for g in range(4):
            k.mm(pm[:, g * 128:(g + 1) * 128], wsT[:, sm, g, :], va_nb[:, g * 128:(g + 1) * 128], True, True)
        for g in range(4):
            k.stt("dve", cat[:, g * 128:(g + 1) * 128], pm[:, g * 128:(g + 1) * 128], bs[:, sm, g:g + 1], ua_g[:, g * 128:(g + 1) * 128],
                  ALU.add, ALU.mult)
        for h in range(4):
            k.tr(pT[:, h, :], qd[:, h, :], ident[:])
            k.tr(pT[:, 4 + h, :], kk[:, h, :], ident[:])
        k.cp("act", qkT[:], pT[:])
        pin = pmi[1][:].re("p (h c) -> p h c", h=4)
        for h in range(4):
            k.mm(pin[:, h, :], qkT[:, 4 + h, :], qkT[:, h, :], True, True)
        k.tt("dve", innT[:], pin, dmk[:, sm], ALU.mult)
        po = pmi[2][:].re("p (h c) -> p h c", h=4)
        if sm:
            for b in range(16):
                k.cp("pool", Zm[:, b, :, b * 8:(b + 1) * 8], qkT[:, 0:4, b * 8:(b + 1) * 8])
        if sm:
            k.mm(pmi[2][:], zeros[:, 0:128], zeros[:], True, True)
            for h in range(4):
                k.mm(po[:, h, :], innT[:, h, :], vb[:, h, :], False, False, skip=True)
            for b in range(16):
                k.cp("act", S0b[b % 2][:], S0f[:, b])
                for h in range(4):
                    k.mm(po[:, h, :], Zm[:, b, h, :], S0b[b % 2][:, h, :], False, b == 15, skip=True)
        for h in range(4):
            if sm:
                pass
            elif ti == 0:
                k.mm(po[:, h, :], innT[:, h, :], vb[:, h, :], True, True)
            else:
                k.mm(po[:, h, :], innT[:, h, :], vb[:, h, :], True, False)
                k.mm(po[:, h, :], qkT[:, h, :], S_b[:, h, :], False, True)
        pS = pmi[3][:].re("p (h c) -> p h c", h=4)
        if sm:
            for b in range(16):
                k.ts("dve", kdm[:], kd[:], rowm[:, b:b + 1], ALU.mult)
                for h in range(4):
                    k.mm(pS[:, h, :], kdm[:, h, :], vb[:, h, :], True, True)
                for h in range(4):
                    k.stt("dve", S0f[:, b, h, :], S0f[:, b, h, :], gam[h] ** 8, pS[:, h, :], ALU.mult, ALU.add)
            k.dma("sp", ors[:].re("b h d e -> d b h e"), S0f[:])
        else:
            for h in range(4):
                k.mm(pS[:, h, :], kd[:, h, :], vb[:, h, :], True, True)
            for h in range(4):
                k.stt("dve", S_f[:, h, :], S_f[:, h, :], gam[h] ** 128, pS[:, h, :], ALU.mult, ALU.add)
            if ti == NT - 2:
                k.dma("sp", orp[:].re("h d e -> d h e"), S_f[:])
            else:
                k.cp("act", S_b[:], S_f[:])
        group_ln(k, po, 4, 128, gvA[:, 2, :].re("p (g c) -> p g c", g=4), gvA[:, 3, :].re("p (g c) -> p g c", g=4),
                 g4(on), g4(tmpA), stA)
        k.tt("dve", cat[:, 512:1024], on[:], gs[:], ALU.mult)
        for kc in range(8):
            k.tr(pT[:, kc, :], cat[:, kc * 128:(kc + 1) * 128], ident[:])
        k.cp("act", catT[:], pT[:])
        for n in range(2):
            for kc in range(8):
                k.mm(ph[n][:], catT[:, kc, :], w_o_s[:, kc, n * 512:(n + 1) * 512], kc == 0, kc == 7)
        final_ln(ti, xs[s][:], [ph[0][:], ph[1][:]], lnA, x1d, preA, tmpL, stL)

    AT = int(os.environ.get("MK_AT", str(NT)))
    if AT > 0:
        loadA(0)
    for ti in range(AT):
        if ti + 1 < AT:
            loadA(ti + 1)
        computeA(ti)
    k.pop()
    if STAGE <= 1:
        k.pop()
        k.barrier()
        return nc

    k.push()
    wq_s = k.sb("wq_s", [128, 8, 1024], BF16)
    wk_s = k.sb("wk_s", [128, 8, 1024], BF16)
    wv_s = k.sb("wv_s", [128, 8, 1024], BF16)
    wo_s = k.sb("wo_s", [128, 8, 1024], BF16)
    load_w(wk_s, ca_wk, 1024)
    load_w(wv_s, ca_wv, 1024)
    load_w(wq_s, ca_wq, 1024)
    load_w(wo_s, ca_wo, 1024)
    mT = k.sb("mT", [128, 8, 256], BF16)
    for a_ in range(2):
        k.dma("pool", mT[:, a_ * 4:(a_ + 1) * 4, :], mTl[:, a_ * 4:(a_ + 1) * 4, :])
    kTm = k.sb("kTm", [128, 8, 256], BF16)
    vbm = k.sb("vbm", [128, 2, 1024], BF16)
    kvout = k.sb("kvout", [128, 2, 2, 1024], F32)
    x1s = [k.sb(f"x1s{i}", [128, 1024], F32) for i in range(2)]
    x1b = k.sb("x1b", [128, 1024], BF16)
    x1T = k.sb("x1T", [128, 8, 128], BF16)
    qT = k.sb("qT", [128, 8, 128], BF16)
    Zq = k.sb("Zq", [128, 16, 8, 128], BF16)
    kTb = [k.sb(f"kTb{i}", [128, 8, 256], BF16) for i in range(2)]
    vbs = [k.sb(f"vbs{i}", [128, 2, 1024], BF16) for i in range(2)]
    mx = k.sb("mx", [128, 4], F32)
    rsum = k.sb("rsum", [128, 4], F32)
    pexp = k.sb("pexp", [128, 4, 256], BF16)
    pTs = k.sb("pTs", [128, 8, 128], BF16)
    cab = k.sb("cab", [128, 4, 256], BF16)
    caT = k.sb("caT", [128, 8, 128], BF16)
    preB = k.sb("preB", [128, 1024], F32)
    tmpB = k.sb("tmpB", [128, 1024], F32)
    stB = k.sb("stB", [128, 4, 1], F32)
    lnB = k.sb("lnB", [128, 2, 1024], F32)
    for i in range(2):
        load_vec(lnB[:, i, :], 6 + i, 1024)
    pT = k.ps("pTb", [128, 8, 128], BF16)
    pq = k.ps("pq", [128, 2, 512], F32)
    psc = k.ps("psc", [128, 2, 512], F32)
    ppv = k.ps("ppv", [128, 2, 512], F32)
    for b_ in range(0, 16, 4):
        k.memset("pool", Zq[:, b_:b_ + 4], 0.0)
    BS = int(os.environ.get("MK_BS", "9"))
    for which, wsb, odr in ((0, wk_s, omk), (1, wv_s, omv)) if BS >= 1 else ():
        for mt in range(2):
            for n in range(2):
                p = pq[:, n, :]
                for kc in range(8):
                    k.mm(p, mT[:, kc, mt * 128:(mt + 1) * 128], wsb[:, kc, n * 512:(n + 1) * 512], kc == 0, kc == 7)
                k.cp("act", kvout[:, which, mt, n * 512:(n + 1) * 512], p)
                if which == 1:
                    k.cp("dve", vbm[:, mt, n * 512:(n + 1) * 512], kvout[:, which, mt, n * 512:(n + 1) * 512])
        k.dma("sp", odr[:].re("(mt p) n -> p mt n", p=128), kvout[:, which])
    for g in range(8 if BS >= 2 else 0):
        p = psc[:, g % 2, 0:256]
        for kc in range(8):
            k.mm(p, wk_s[:, kc, g * 128:(g + 1) * 128], mT[:, kc, :], kc == 0, kc == 7)
        k.cp("act", kTm[:, g, :], p)

    def loadB(ti):
        k.dma("sp", x1s[ti % 2][:], x1d[ti])

    def computeB(ti):
        s = ti % 2
        sm = ti == NT - 1
        k.cp("act", x1b[:], x1s[s][:])
        for kc in range(8):
            k.tr(pT[:, kc, :], x1b[:, kc * 128:(kc + 1) * 128], ident[:])
        k.cp("dve", x1T[:], pT[:])
        pq8 = pq[:].re("p a (g t) -> p (a g) t", g=4)
        for g in range(8):
            for kc in range(8):
                k.mm(pq8[:, g, :], wq_s[:, kc, g * 128:(g + 1) * 128], x1T[:, kc, :], kc == 0, kc == 7)
        k.ts("dve", qT[:], pq8, 1.0 / 16.0, ALU.mult)
        sc = psc[:].re("p a (h m) -> p (a h) m", h=2)
        pv = ppv[:].re("p a (h m) -> p (a h) m", h=2)
        if not sm:
            for h in range(4):
                for dc in range(2):
                    k.mm(sc[:, h, :], qT[:, h * 2 + dc, :], kTm[:, h * 2 + dc, :], dc == 0, dc == 1)
        else:
            for a in range(2):
                k.mm(psc[:, a, :], zeros[:, 0:128], zeros[:], True, True)
            for b in range(16):
                k.cp("pool", Zq[:, b, :, b * 8:(b + 1) * 8], qT[:, :, b * 8:(b + 1) * 8])
            def ldk(b_):
                for a_ in range(2):
                    k.dma("pool", kTb[b_ % 2][:, a_ * 4:(a_ + 1) * 4, :], ckT[b_][:, a_ * 4:(a_ + 1) * 4, :])
            ldk(0)
            for b in range(16):
                if b + 1 < 16:
                    ldk(b + 1)
                for g in range(8):
                    k.mm(sc[:, g // 2, :], Zq[:, b, g, :], kTb[b % 2][:, g, :], False, (b == 15 and g % 2 == 1), skip=True)
        k.red(mx[:], sc, ALU.max)
        k.ts("dve", mx[:], mx[:], -1.0, ALU.mult)
        for h in range(4):
            k.act(pexp[:, h, :], sc[:, h, :], AF.Exp, bias=mx[:, h:h + 1], scale=1.0, accum=rsum[:, h:h + 1])
        k.recip(rsum[:], rsum[:])
        for h in range(4):
            for mc in range(2):
                k.tr(pT[:, h * 2 + mc, :], pexp[:, h, mc * 128:(mc + 1) * 128], ident[:])
        k.cp("dve", pTs[:], pT[:])
        if not sm:
            for h in range(4):
                for mc in range(2):
                    k.mm(pv[:, h, :], pTs[:, h * 2 + mc, :], vbm[:, mc, h * 256:(h + 1) * 256], mc == 0, mc == 1)
        else:
            for a in range(2):
                k.mm(ppv[:, a, :], zeros[:, 0:128], zeros[:], True, True)
            for b in range(16):
                k.cp("pool", Zq[:, b, :, b * 8:(b + 1) * 8], pTs[:, :, b * 8:(b + 1) * 8])
            def ldv(b_):
                for a_ in range(2):
                    k.dma("pool", vbs[b_ % 2][:, a_, :], cvl[b_][:, a_, :])
            ldv(0)
            for b in range(16):
                if b + 1 < 16:
                    ldv(b + 1)
                for h in range(4):
                    for mc in range(2):
                        k.mm(pv[:, h, :], Zq[:, b, h * 2 + mc, :], vbs[b % 2][:, mc, h * 256:(h + 1) * 256], False,
                             (b == 15 and mc == 1), skip=True)
        k.tt("dve", cab[:], pv, rsum[:].uq(2).bc([128, 4, 256]), ALU.mult)
        cab2 = cab[:].re("p h m -> p (h m)")
        for kc in range(8):
            k.tr(pT[:, kc, :], cab2[:, kc * 128:(kc + 1) * 128], ident[:])
        k.cp("act", caT[:], pT[:])
        pco = pq[:].re("p a c -> p (a c)")
        for n in range(2):
            for kc in range(8):
                k.mm(pco[:, n * 512:(n + 1) * 512], caT[:, kc, :], wo_s[:, kc, n * 512:(n + 1) * 512], kc == 0, kc == 7)
        final_ln(ti, x1s[s][:], [pco[:, 0:512], pco[:, 512:1024]], lnB, x2d, preB, tmpB, stB)

    BT = int(os.environ.get("MK_BT", str(NT)))
    tilesB = list(range(NT)) if BT >= NT else ([NT - 1] if BT < 0 else list(range(BT)))
    if tilesB:
        loadB(tilesB[0])
    for i_, ti in enumerate(tilesB):
        if i_ + 1 < len(tilesB):
            loadB(tilesB[i_ + 1])
        computeB(ti)
    k.pop()
    if STAGE <= 2:
        k.pop()
        k.barrier()
        return nc

    x2T = k.sb("x2T", [128, 8, NT * 128], BF16)
    tau = k.sb("tau", [128, NT, 8], F32)
    ebias = k.sb("ebias", [128, NT, 8], F32)
    k.push()
    pwq = k.sb("pwq", [128, 8, 2048], BF16)
    load_w(pwq, peer_wq, 2048)
    skT = k.sb("skT", [128, 16, 128], BF16)
    for a_ in range(2):
        k.dma("pool", skT[:, a_ * 8:(a_ + 1) * 8, :], skTl[:, a_ * 8:(a_ + 1) * 8, :])
    x2s = [k.sb(f"x2s{i}", [128, 1024], F32) for i in range(2)]
    x2b = k.sb("x2b", [128, 1024], BF16)
    qpT = k.sb("qpT", [128, 16, 128], BF16)
    s_sbs = [k.sb(f"s_sb{i}", [128, 16, 128], F32) for i in range(2)]
    s_tmp = k.sb("s_tmp", [128, 16, 128], F32)
    sv = k.sb("sv", [128, 16, 16], F32)
    cand = k.sb("cand", [128, 8, 256], F32)
    cand2 = k.sb("cand2", [128, 8, 256], F32)
    fv = k.sb("fv", [128, 8, 16], F32)
    fe = k.sb("fe", [128, 8, 16], F32)
    fv3 = k.sb("fv3", [128, 8, 8], F32)
    zs = k.sb("zs", [128, 8], F32)
    pT = k.ps("pTc", [128, 8, 128], BF16)
    pqp = k.ps("pqp", [128, 4, 512], F32)
    pss = pqp

    def loadC(ti):
        k.dma("sp", x2s[ti % 2][:], x2d[ti])

    def computeC1(ti):
        s = ti % 2
        s_sb = s_sbs[s]
        k.cp("act", x2b[:], x2s[s][:])
        for kc in range(8):
            k.tr(pT[:, kc, :], x2b[:, kc * 128:(kc + 1) * 128], ident[:])
        k.cp("act", x2T[:, :, ti * 128:(ti + 1) * 128], pT[:])
        for hq in range(4):
            for j in range(4):
                hc = hq * 4 + j
                for kc in range(8):
                    k.mm(pqp[:, hq, j * 128:(j + 1) * 128], pwq[:, kc, hc * 128:(hc + 1) * 128], x2T[:, kc, ti * 128:(ti + 1) * 128],
                         kc == 0, kc == 7)
            k.cp("act", qpT[:, hq * 4:(hq + 1) * 4, :], pqp[:, hq, :].re("p (j t) -> p j t", j=4))
        for hc in range(16):
            k.mm(pss[:, hc // 4, (hc % 4) * 128:(hc % 4 + 1) * 128], qpT[:, hc, :], skT[:, hc, :], True, True)
        k.cp("act", s_sb[:], pss[:].re("p a (j t) -> p (a j) t", j=4))
        k.dma("sp", sd[ti].re("p (a t) -> p a t", a=16), s_sb[:])

    def computeC2(ti):
        s_sb = s_sbs[ti % 2]
        for hc in range(16):
            k.max8(sv[:, hc, 0:8], s_sb[:, hc, :])
            k.mrep(s_tmp[:, hc, :], sv[:, hc, 0:8], s_sb[:, hc, :], -1e30)
            k.max8(sv[:, hc, 8:16], s_tmp[:, hc, :])
        sv4 = sv[:].re("p (h c) r -> p h c r", c=2)
        c4 = cand[:].re("p h (a b) -> p h a b", a=16)
        k.tt("dve", c4, sv4[:, :, 0, :].uq(3).bc([128, 8, 16, 16]), sv4[:, :, 1, :].uq(2).bc([128, 8, 16, 16]), ALU.add)
        for h in range(8):
            k.max8(fv[:, h, 0:8], cand[:, h, :])
            k.mrep(cand2[:, h, :], fv[:, h, 0:8], cand[:, h, :], -1e30)
            k.max8(fv[:, h, 8:16], cand2[:, h, :])
            k.mrep(cand2[:, h, :], fv[:, h, 8:16], cand2[:, h, :], -1e30)
            k.max8(fv3[:, h, :], cand2[:, h, :])
        k.tt("dve", tau[:, ti, :], fv[:, :, 15], fv3[:, :, 0], ALU.add)
        k.ts("dve", tau[:, ti, :], tau[:, ti, :], 0.5, ALU.mult)
        k.tt("dve", fe[:], fv[:], fv[:, :, 0:1].bc([128, 8, 16]), ALU.subtract)
        k.act(fe[:], fe[:], AF.Exp)
        k.red(zs[:], fe[:], ALU.add)
        k.act(zs[:], zs[:], AF.Ln)
        k.stt("dve", ebias[:, ti, :], fv[:, :, 0], -1.0, zs[:], ALU.mult, ALU.subtract)

    loadC(0)
    loadC(1)
    computeC1(0)
    for ti in range(NT):
        if ti + 2 < NT:
            loadC(ti + 2)
        if ti + 1 < NT:
            computeC1(ti + 1)
        computeC2(ti)
    k.pop()
    if STAGE <= 3:
        k.pop()
        k.barrier()
        return nc

    groups = [list(range(g, min(g + 3, NT))) for g in range(0, NT, 3)]
    k.push()
    GM = 384
    WT = k.sb("WT", [128, 128, GM], BF16)
    stD = k.sb("stD", [128, 4, 1], F32)
    lnD = k.sb("lnD", [128, 2, 1024], F32)
    for i in range(2):
        load_vec(lnD[:, i, :], 8 + i, 1024)
    NB = 4
    POOLSET = [int(c_) for c_ in os.environ.get("MK_POOLSET", "00010001")]
    for grp in groups:
        n = len(grp)
        G = n * 128
        col0 = grp[0] * 128
        k.push()
        ssl = [k.sb(f"ssl{i}", [128, 16, 128], F32) for i in range(2)]
        s0p = [k.sb(f"s0p{i}", [128, 8, 128], F32) for i in range(2)]
        bias2 = [k.sb(f"bias2{i}", [128, 8], F32) for i in range(2)]
        cb = [k.sb(f"cb{i}", [128, 8, 128], BF16) for i in range(NB)]
        eb = [k.sb(f"eb{i}", [128, 8, 128], BF16) for i in range(NB)]
        wm = [k.sb(f"wm{i}", [128, 8, 128], BF16) for i in range(NB)]
        wsum = [k.sb(f"wsum{i}", [128, 1024], BF16) for i in range(2)]
        pacc = [k.ps(f"pacc{i}", [128, 2, 512], F32) for i in range(2)]
        ptr = [k.ps(f"ptr{i}", [128, 8, 128], BF16) for i in range(2)]
        k.dma("sp", ssl[grp[0] % 2][:], sd[grp[0]].re("p (a t) -> p a t", a=16))
        steps = [(tt, ti, ib, h) for tt, ti in enumerate(grp) for ib in range(16) for h in range(8)]
        NS = len(steps)

        def stA(kk):
            tt, ti, ib, h = steps[kk]
            if ib == 0 and h == 0:
                if tt + 1 < n:
                    k.dma("sp", ssl[grp[tt + 1] % 2][:], sd[grp[tt + 1]].re("p (a t) -> p a t", a=16))
                s4_ = ssl[ti % 2][:].re("p (h c) t -> p h c t", c=2)
                k.tt("dve", s0p[ti % 2][:], s4_[:, :, 0, :], tau[:, ti, :].uq(2).bc([128, 8, 128]), ALU.subtract)
                k.tt("dve", bias2[ti % 2][:], tau[:, ti, :], ebias[:, ti, :], ALU.add)
            s4_ = ssl[ti % 2][:].re("p (h c) t -> p h c t", c=2)
            k.tt("pool" if POOLSET[kk % 8] else "dve", cb[kk % NB][:], s0p[ti % 2][:, h, ib * 8:(ib + 1) * 8].uq(2).bc([128, 8, 128]),
                 s4_[:, h, 1, :].uq(1).bc([128, 8, 128]), ALU.add)

        BIGNEG = 1.0e9

        def stB1(kk):
            if kk % 4 != 3:
                k.act(eb[kk % NB][:], cb[kk % NB][:], AF.Prelu, alpha=BIGNEG)
            else:
                k.stt("dve", eb[kk % NB][:], cb[kk % NB][:], BIGNEG, cb[kk % NB][:], ALU.mult, ALU.min)

        def stB(kk):
            tt, ti, ib, h = steps[kk]
            k.act(wm[kk % NB][:], eb[kk % NB][:], AF.Exp, bias=bias2[ti % 2][:, h:h + 1], scale=1.0)

        def stC(kk):
            tt, ti, ib, h = steps[kk]
            blk = kk // 8
            acc = pacc[blk % 2]
            w2 = wm[kk % NB][:].re("p i j -> p (i j)")
            for hf in range(2):
                k.mm(acc[:, hf, :], ident[:], w2[:, hf * 512:(hf + 1) * 512], h == 0, h == 7)

        def post(blk):
            tt, ti, ib, h = steps[blk * 8]
            ws = wsum[blk % 2]
            k.cp("dve", ws[:], pacc[blk % 2][:].re("p a c -> p (a c)"))
            pt_ = ptr[blk % 2]
            for i in range(8):
                k.tr(pt_[:, i, :], ws[:, i * 128:(i + 1) * 128], ident[:])
            k.cp("dve", WT[:, ib * 8:(ib + 1) * 8, tt * 128:(tt + 1) * 128], pt_[:])

        pend = []
        for kk in range(NS + 7):
            if kk < NS:
                stA(kk)
            if 0 <= kk - 1 < NS:
                stB1(kk - 1)
            if 0 <= kk - 2 < NS:
                stB(kk - 2)
            if 0 <= kk - 3 < NS:
                stC(kk - 3)
                if (kk - 3) % 8 == 7:
                    pend.append(((kk - 3) // 8, kk + 3))
            while pend and pend[0][1] <= kk:
                post(pend.pop(0)[0])
        assert not pend
        k.pop()
        k.push()
        UTs = [k.sb(f"UTs{i}", [128, 8, 128], BF16) for i in range(3)]
        Vs = [k.sb(f"Vs{i}", [128, 1024], BF16) for i in range(3)]
        ga = [k.sb(f"ga{i}", [128, GM], BF16) for i in range(2)]
        wa = [k.sb(f"wa{i}", [128, GM], BF16) for i in range(2)]
        x2r = k.sb("x2r", [128, 1024], F32)
        preD = k.sb("preD", [128, 1024], F32)
        tmpD = k.sb("tmpD", [128, 1024], F32)
        pA = [k.ps(f"pA{i}", [128, 512], F32) for i in range(2)]
        pf = k.ps("pf", [128, 3, 1024], F32)

        def loadD(i):
            k.dma("pool", UTs[i % 3][:], UTl[i])
            k.dma("pool", Vs[i % 3][:], peer_v[i])

        def mm1(i):
            for kc in range(8):
                k.mm(pA[i % 2][:, 0:G], UTs[i % 3][:, kc, :], x2T[:, kc, col0:col0 + G], kc == 0, kc == 7)

        loadD(0)
        loadD(1)
        mm1(0)
        for i in range(128):
            if i + 2 < 128:
                loadD(i + 2)
            k.act(ga[i % 2][:, 0:G], pA[i % 2][:, 0:G], AF.Gelu_apprx_tanh)
            k.tt("dve", wa[i % 2][:, 0:G], ga[i % 2][:, 0:G], WT[:, i, 0:G], ALU.mult)
            if i + 1 < 128:
                mm1(i + 1)
            for tt in range(n):
                for hf in range(2):
                    k.mm(pf[:, tt, hf * 512:(hf + 1) * 512], wa[i % 2][:, tt * 128:(tt + 1) * 128], Vs[i % 3][:, hf * 512:(hf + 1) * 512],
                         i == 0, i == 127)
        for tt, ti in enumerate(grp):
            k.dma("sp", x2r[:], x2d[ti])
            final_ln(ti, x2r[:], [pf[:, tt, 0:512], pf[:, tt, 512:1024]], lnD, y, preD, tmpD, stD)
        k.pop()
    k.pop()
    k.pop()
    k.barrier()
    return nc


def _kc_layout(w):
    K_, N = w.shape
    return np.ascontiguousarray(w.reshape(K_ // 128, 128, N).transpose(1, 0, 2))


def _consts():
    H = 4
    gam = 1.0 - 2.0 ** (-5.0 - np.arange(H, dtype=np.float64))
    half = 64
    inv = 1.0 / (10000.0 ** (np.arange(half, dtype=np.float64) / half))
    rope = np.zeros((NT, 128, 4, 256), np.float64)
    dmaskT = np.zeros((2, 128, 4, 128), np.float64)
    kdec = np.zeros((2, 128, 4), np.float64)
    for ti in range(NT):
        if ti < NT - 1:
            pos = ti * 128 + np.arange(128)
            loc = np.arange(128)
        else:
            loc = np.arange(128) % 8
            pos = 16384 + loc
        ang = np.float32(pos.astype(np.float32)[:, None] * inv.astype(np.float32)[None, :]).astype(np.float64)
        c, s = np.cos(ang), np.sin(ang)
        qdec = gam[None, :] ** (loc[:, None] + 1.0)
        rope[ti, :, 0] = (c[:, None, :] * qdec[:, :, None]).reshape(128, 256)
        rope[ti, :, 1] = (s[:, None, :] * qdec[:, :, None]).reshape(128, 256)
        rope[ti, :, 2] = np.tile(c * 128 ** -0.5, (1, 4))
        rope[ti, :, 3] = np.tile(s * 128 ** -0.5, (1, 4))
    j = np.arange(128)
    for h in range(H):
        m = (j[None, :] >= j[:, None]).astype(np.float64)
        dmaskT[0, :, h, :] = m * gam[h] ** (-(j[:, None] + 1.0))
        jl, bl = j % 8, j // 8
        ms = ((bl[None, :] == bl[:, None]) & (jl[None, :] >= jl[:, None])).astype(np.float64)
        dmaskT[1, :, h, :] = ms * gam[h] ** (-(jl[:, None] + 1.0))
        kdec[0, :, h] = gam[h] ** (127.0 - j)
        kdec[1, :, h] = gam[h] ** (7.0 - jl)
    wsmask = np.zeros((2, 128, 4, 128), np.float32)
    wsmask[0] = (j[None, :] >= j[:, None]).astype(np.float32)[:, None, :]
    wsmask[1] = wsmask[0]
    return rope.astype(np.float32), dmaskT.astype(np.float32), kdec.astype(np.float32), wsmask


_NC_CACHE = {}


def kernel(**inp):
    f = lambda a: np.ascontiguousarray(np.asarray(a, dtype=np.float32))
    rope, dmaskT, kdec, wsmask = _consts()
    ident = np.eye(128, dtype=np.float32).astype(ml_dtypes.bfloat16)
    w_s = f(inp["w_s"])[0]
    b_s = f(inp["b_s"])[0]
    wsTl = np.zeros((2, 128, 4, 128), np.float32)
    wsTl[0] = w_s.transpose(2, 0, 1)
    bsl = np.zeros((2, 128, 4), np.float32)
    bsl[0] = b_s.T
    for b in range(16):
        wsTl[1, b * 8:(b + 1) * 8, :, b * 8:(b + 1) * 8] = w_s[:, :8, :8].transpose(2, 0, 1)
        bsl[1, b * 8:(b + 1) * 8, :] = b_s[:, :8].T
    vecs = np.zeros((10, 1024), np.float32)
    vecs[0, :512] = f(inp["gate_ln_g"])[0].reshape(-1)
    vecs[1, :512] = f(inp["gate_ln_b"])[0].reshape(-1)
    vecs[2, :512] = f(inp["ret_gn_g"])[0].reshape(-1)
    vecs[3, :512] = f(inp["ret_gn_b"])[0].reshape(-1)
    for i, nm in enumerate(["ln1_g", "ln1_b", "ln2_g", "ln2_b", "ln3_g", "ln3_b"]):
        vecs[4 + i] = f(inp[nm])[0]
    shared = {
        "w_in": _kc_layout(f(inp["w_in"])[0]), "w_o": _kc_layout(f(inp["w_o"])[0]),
        "ca_wq": _kc_layout(f(inp["ca_wq"])[0]), "ca_wk": _kc_layout(f(inp["ca_wk"])[0]),
        "ca_wv": _kc_layout(f(inp["ca_wv"])[0]), "ca_wo": _kc_layout(f(inp["ca_wo"])[0]),
        "peer_wq": _kc_layout(f(inp["peer_wq"])[0]),
        "skTl": np.ascontiguousarray(f(inp["peer_subkeys"])[0].reshape(16, 128, 128).transpose(2, 0, 1)),
        "UTl": np.ascontiguousarray(f(inp["peer_u"])[0].reshape(128, 128, 8, 128).transpose(0, 3, 2, 1)),
        "peer_v": f(inp["peer_v"])[0].reshape(128, 128, 1024),
        "wsTl": wsTl, "wsmask": wsmask, "bsl": bsl, "vecs": vecs, "rope": rope, "dmaskT": dmaskT, "kdec": kdec, "ident": ident,
    }
    if STAGE < 4:
        shared["UTl"] = shared["UTl"][:1]
        shared["peer_v"] = shared["peer_v"][:1]
    xp, xsm = f(inp["x_prompt"]), f(inp["x_sample"])
    memp = f(inp["mem_prompt"])
    cmk, cmv, sr = f(inp["cache_mem_k"])[0], f(inp["cache_mem_v"])[0], f(inp["state_ret"])[0]
    in_maps = []
    for c in range(8):
        xin = np.concatenate([xp[c].reshape(16, 128, 1024), xsm[16 * c:16 * c + 16].reshape(1, 128, 1024)], axis=0)
        m = dict(shared)
        m["xin"] = np.ascontiguousarray(xin)
        m["xinT"] = np.ascontiguousarray(xin.reshape(NT, 128, 8, 128).transpose(0, 3, 2, 1))
        m["mTl"] = np.ascontiguousarray(memp[c].reshape(256, 8, 128).transpose(2, 1, 0))
        m["ckT"] = np.ascontiguousarray(cmk[16 * c:16 * c + 16].reshape(16, 256, 8, 128).transpose(0, 3, 2, 1))
        m["cvl"] = np.ascontiguousarray(cmv[16 * c:16 * c + 16].reshape(16, 2, 128, 1024).transpose(0, 2, 1, 3))
        m["sret"] = np.ascontiguousarray(sr[16 * c:16 * c + 16].transpose(2, 0, 1, 3))
        in_maps.append(m)
    if "nc" not in _NC_CACHE:
        _NC_CACHE["nc"] = build()
    nc = _NC_CACHE["nc"]
    res = run_bass_kernel_spmd(nc, in_maps, core_ids=list(range(8)))
    R = res.results
    _NC_CACHE["last"] = R
    g = lambda name: [np.asarray(r[name], dtype=np.float32) for r in R]
    ys = g("y")
    y_prompt = np.stack([a[:16].reshape(2048, 1024) for a in ys])
    y_sample = np.concatenate([a[16].reshape(16, 8, 1024) for a in ys], axis=0)
    new_mem_k = np.stack([a.reshape(256, 4, 256) for a in g("omk")])[None]
    new_mem_v = np.stack([a.reshape(256, 4, 256) for a in g("omv")])[None]
    new_ret_prompt = np.stack(g("orp"))[None]
    new_ret_sample = np.concatenate(g("ors"), axis=0)[None]
    new_gate_v = np.concatenate([a.reshape(16, 8, 4, 128) for a in g("ogv")], axis=0)[None]
    return (y_prompt, y_sample, new_mem_k, new_mem_v, new_ret_prompt, new_ret_sample, new_gate_v)
```
